# Optimizing a Trainium2 kernel written in Bass

```python
import jax, jax.numpy as jnp
from jax import lax
import numpy as np

D_MODEL = 1024
BATCH = 8
SEQ = 2048
DEPTH = 1
DEC_BATCH = 8
DEC_SEQ = 8192
PAST_LEN = 128

MIX_WIDTH = D_MODEL
ATTN_WIDTH = MIX_WIDTH // 2
CONV_WIDTH = MIX_WIDTH - ATTN_WIDTH
HEAD_DIM = 64
N_HEADS = ATTN_WIDTH // HEAD_DIM
DILATED_BRANCHES = ((128, 1), (512, 4), (2048, 16))
CONV_KERNEL = 31
D_FF = 2816
FFN_RESIDUAL_WEIGHT = 0.5
IN_PROJ_WIDTH = 3 * ATTN_WIDTH + 2 * CONV_WIDTH
RMS_EPS = 1e-6
LN_EPS = 1e-5
NEG_INF = -1e30

kernel_name = "hymba_longnet_conformer_encoder"


def rms_norm(x, g):
    x32 = x.astype(jnp.float32)
    y = x32 * lax.rsqrt(jnp.mean(x32 * x32, axis=-1, keepdims=True) + RMS_EPS)
    return (y * g.astype(jnp.float32)).astype(x.dtype)


def layer_norm(x, g, b):
    x32 = x.astype(jnp.float32)
    mu = jnp.mean(x32, axis=-1, keepdims=True)
    var = jnp.mean(jnp.square(x32 - mu), axis=-1, keepdims=True)
    y = (x32 - mu) * lax.rsqrt(var + LN_EPS)
    return (y * g.astype(jnp.float32) + b.astype(jnp.float32)).astype(x.dtype)


def alibi_slopes(n_heads):
    return 2.0 ** (-8.0 * jnp.arange(1, n_heads + 1, dtype=jnp.float32) / n_heads)


def swiglu(h, w_gu, w_down):
    gate, up = jnp.split(h @ w_gu, 2, axis=-1)
    return (jax.nn.silu(gate) * up) @ w_down


def dilated_branch(q, k, v, slopes, dilation, half):
    B, S, H, Dh = q.shape
    L = S // dilation
    N = B * dilation
    nb = -(-L // half)
    Lp = nb * half

    def strided(a):
        return a.reshape(B, L, dilation, H, Dh).transpose(0, 2, 1, 3, 4).reshape(N, L, H, Dh)

    qs = jnp.pad(strided(q), ((0, 0), (0, Lp - L), (0, 0), (0, 0))).reshape(N, nb, half, H, Dh)

    def key_blocks(a):
        ap = jnp.pad(strided(a), ((0, 0), (half, Lp - L + half), (0, 0), (0, 0))).reshape(N, nb + 2, half, H, Dh)
        return jnp.concatenate([ap[:, :-2], ap[:, 1:-1], ap[:, 2:]], axis=2)

    kb, vb = key_blocks(k), key_blocks(v)
    s = jnp.einsum('nbqhd,nbkhd->nbhqk', qs, kb).astype(jnp.float32)
    rel = jnp.arange(3 * half)[None, :] - half - jnp.arange(half)[:, None]
    kpos = jnp.arange(nb)[:, None] * half - half + jnp.arange(3 * half)[None, :]
    valid = (jnp.abs(rel) <= half)[None, :, :] & ((kpos >= 0) & (kpos < L))[:, None, :]
    bias = -(slopes * dilation)[:, None, None] * jnp.abs(rel).astype(jnp.float32)[None]
    s = jnp.where(valid[None, :, None], s + bias[None, None], NEG_INF)
    m = jnp.max(s, axis=-1, keepdims=True)
    p = jnp.exp(s - m)
    den = jnp.sum(p, axis=-1, keepdims=True)
    o = jnp.einsum('nbhqk,nbkhd->nbqhd', (p / den).astype(v.dtype), vb)
    lse = jnp.transpose((m + jnp.log(den))[..., 0], (0, 1, 3, 2))
    o = o.reshape(N, Lp, H, Dh)[:, :L].reshape(B, dilation, L, H, Dh).transpose(0, 2, 1, 3, 4).reshape(B, S, H, Dh)
    lse = lse.reshape(N, Lp, H)[:, :L].reshape(B, dilation, L, H).transpose(0, 2, 1, 3).reshape(B, S, H)
    return o, lse


def dilated_attention(q, k, v):
    slopes = alibi_slopes(N_HEADS)
    outs, lses = [], []
    for window, dilation in DILATED_BRANCHES:
        o, lse = dilated_branch(q, k, v, slopes, dilation, window // (2 * dilation))
        outs.append(o.astype(jnp.float32))
        lses.append(lse)
    w = jax.nn.softmax(jnp.stack(lses), axis=0)
    o = jnp.sum(w[..., None] * jnp.stack(outs), axis=0)
    return o.astype(q.dtype)


def mixer(h, w_in, conv_w, conv_b, ln_g, ln_b, w_out):
    B, S, _ = h.shape
    proj = h @ w_in
    q, k, v, ga, gb = jnp.split(proj, [ATTN_WIDTH, 2 * ATTN_WIDTH, 3 * ATTN_WIDTH, 3 * ATTN_WIDTH + CONV_WIDTH], axis=-1)
    q = q.reshape(B, S, N_HEADS, HEAD_DIM) * (HEAD_DIM ** -0.5)
    k = k.reshape(B, S, N_HEADS, HEAD_DIM)
    v = v.reshape(B, S, N_HEADS, HEAD_DIM)
    attn = dilated_attention(q, k, v).reshape(B, S, ATTN_WIDTH)
    glu = ga * jax.nn.sigmoid(gb)
    pad = CONV_KERNEL // 2
    c = lax.conv_general_dilated(glu, conv_w.astype(glu.dtype), window_strides=(1,), padding=[(pad, pad)],
                                 dimension_numbers=('NWC', 'WIO', 'NWC'), feature_group_count=CONV_WIDTH) + conv_b
    c = jax.nn.silu(layer_norm(c, ln_g, ln_b))
    return jnp.concatenate([attn, c], axis=-1) @ w_out


def trunk(x, ffn1_pre_g, ffn1_w_gu, ffn1_w_down, ffn1_post_g, mix_pre_g, w_in, conv_w, conv_b, conv_ln_g,
          conv_ln_b, w_out, mix_post_g, ffn2_pre_g, ffn2_w_gu, ffn2_w_down, ffn2_post_g, final_g):
    for l in range(DEPTH):
        x = x + FFN_RESIDUAL_WEIGHT * rms_norm(swiglu(rms_norm(x, ffn1_pre_g[l]), ffn1_w_gu[l], ffn1_w_down[l]), ffn1_post_g[l])
        x = x + rms_norm(mixer(rms_norm(x, mix_pre_g[l]), w_in[l], conv_w[l], conv_b[l], conv_ln_g[l], conv_ln_b[l], w_out[l]), mix_post_g[l])
        x = x + FFN_RESIDUAL_WEIGHT * rms_norm(swiglu(rms_norm(x, ffn2_pre_g[l]), ffn2_w_gu[l], ffn2_w_down[l]), ffn2_post_g[l])
        x = rms_norm(x, final_g[l])
    return x


def setup_inputs(seed: int = 0) -> dict:
    key = jax.random.key(seed)
    ks = jax.random.split(key, 24)
    f32 = jnp.float32

    def nrm(k, shape, scale):
        return jax.random.normal(k, shape, f32) * scale

    def gain(k, n):
        return 1.0 + 0.1 * jax.random.normal(k, (DEPTH, n), f32)

    return {
        "x_prompt": jax.random.normal(ks[0], (BATCH, SEQ, D_MODEL), f32),
        "x_sample": jax.random.normal(ks[1], (DEC_BATCH, DEC_SEQ, D_MODEL), f32),
        "ffn1_pre_g": gain(ks[2], D_MODEL),
        "ffn1_w_gu": nrm(ks[3], (DEPTH, D_MODEL, 2 * D_FF), D_MODEL ** -0.5),
        "ffn1_w_down": nrm(ks[4], (DEPTH, D_FF, D_MODEL), D_FF ** -0.5),
        "ffn1_post_g": gain(ks[5], D_MODEL),
        "mix_pre_g": gain(ks[6], D_MODEL),
        "w_in": nrm(ks[7], (DEPTH, D_MODEL, IN_PROJ_WIDTH), D_MODEL ** -0.5),
        "conv_w": nrm(ks[8], (DEPTH, CONV_KERNEL, 1, CONV_WIDTH), CONV_KERNEL ** -0.5),
        "conv_b": nrm(ks[9], (DEPTH, CONV_WIDTH), 0.02),
        "conv_ln_g": gain(ks[10], CONV_WIDTH),
        "conv_ln_b": nrm(ks[11], (DEPTH, CONV_WIDTH), 0.02),
        "w_out": nrm(ks[12], (DEPTH, MIX_WIDTH, D_MODEL), MIX_WIDTH ** -0.5),
        "mix_post_g": gain(ks[13], D_MODEL),
        "ffn2_pre_g": gain(ks[14], D_MODEL),
        "ffn2_w_gu": nrm(ks[15], (DEPTH, D_MODEL, 2 * D_FF), D_MODEL ** -0.5),
        "ffn2_w_down": nrm(ks[16], (DEPTH, D_FF, D_MODEL), D_FF ** -0.5),
        "ffn2_post_g": gain(ks[17], D_MODEL),
        "final_g": gain(ks[18], D_MODEL),
    }


def reference(x_prompt, x_sample, ffn1_pre_g, ffn1_w_gu, ffn1_w_down, ffn1_post_g, mix_pre_g, w_in, conv_w,
              conv_b, conv_ln_g, conv_ln_b, w_out, mix_post_g, ffn2_pre_g, ffn2_w_gu, ffn2_w_down, ffn2_post_g,
              final_g):
    y_prompt = trunk(x_prompt, ffn1_pre_g, ffn1_w_gu, ffn1_w_down, ffn1_post_g, mix_pre_g, w_in, conv_w, conv_b,
                     conv_ln_g, conv_ln_b, w_out, mix_post_g, ffn2_pre_g, ffn2_w_gu, ffn2_w_down, ffn2_post_g, final_g)
    y_sample = trunk(x_sample, ffn1_pre_g, ffn1_w_gu, ffn1_w_down, ffn1_post_g, mix_pre_g, w_in, conv_w, conv_b,
                     conv_ln_g, conv_ln_b, w_out, mix_post_g, ffn2_pre_g, ffn2_w_gu, ffn2_w_down, ffn2_post_g, final_g)
    return (y_prompt, y_sample)
```

```python
import contextlib
import numpy as np
import ml_dtypes
import concourse.bass as bass
import concourse.mybir as mybir
from concourse.bass_utils import run_bass_kernel_spmd

F32 = mybir.dt.float32
BF16 = mybir.dt.bfloat16
ALU = mybir.AluOpType
AF = mybir.ActivationFunctionType
AX = mybir.AxisListType

D = 1024
DFF = 2816
NFC = DFF // 128
KC = 8
TT = 512
NSUB = 4
NH = 8
INW = 2560
CK = 31
RMS_EPS = 1e-6
LN_EPS = 1e-5
DILS = (1, 4, 16)
SB = 2048
MASKV = -30000.0


def sl_(start, n, d):
    return slice(start, start + (n - 1) * d + 1, d)


class Buf:
    __slots__ = ("ap", "w", "r")

    def __init__(self, ap=None):
        self.ap = ap
        self.w = None
        self.r = {}


class Prog:
    ENG = ("sync", "act", "dve", "pool", "pe")

    def __init__(self, nc, es, nds=32):
        self.nc = nc
        self.q = {e: [] for e in self.ENG}
        self.sem = {}
        self.cnt = {}
        for e in ("act", "dve", "pool", "pe"):
            self.sem[e] = es.enter_context(nc.semaphore("s_" + e))
            self.cnt[e] = 0
        self.NDS = nds
        self.NDP = 8
        for i in range(nds):
            k = "d%d" % i
            self.sem[k] = es.enter_context(nc.semaphore("s_" + k))
            self.cnt[k] = 0
        for i in range(self.NDP):
            k = "g%d" % i
            self.sem[k] = es.enter_context(nc.semaphore("s_" + k))
            self.cnt[k] = 0
        self.rrp = 0
        self.known = {e: {} for e in self.ENG}
        self.rr = 0
        self.pe_pending = False
        self.out_dma = []

    def _waits(self, eng, reads, writes, extra=()):
        w = {}

        def need(k, v):
            if v > w.get(k, 0):
                w[k] = v

        for b in reads:
            if b.w is not None:
                need(*b.w)
        for b in writes:
            if b.w is not None:
                need(*b.w)
            for k, v in b.r.items():
                need(k, v)
        for k, v in extra:
            need(k, v)
        out = []
        kn = self.known[eng]
        for k, v in w.items():
            if k == eng and eng == "pe":
                continue
            if kn.get(k, 0) >= v:
                continue
            kn[k] = v
            out.append((k, v))
        return out

    def op(self, eng, fn, reads=(), writes=(), sig=True):
        assert sig or eng == "pe"
        wl = self._waits(eng, reads, writes)
        if sig:
            self.cnt[eng] += 1
            tk = self.cnt[eng]
            if eng == "pe":
                self.pe_pending = False
        else:
            tk = self.cnt[eng] + 1
            self.pe_pending = True
        for b in reads:
            if b.r.get(eng, 0) < tk:
                b.r[eng] = tk
        for b in writes:
            b.w = (eng, tk)
            b.r = {}
        self.q[eng].append((wl, fn, True if sig else None))

    def dma(self, qeng, out_ap, in_ap, reads=(), writes=(), is_out=False):
        if qeng == "pool":
            i = self.rrp
            self.rrp = (i + 1) % self.NDP
            key = "g%d" % i
        else:
            i = self.rr
            self.rr = (i + 1) % self.NDS
            key = "d%d" % i
        prev = self.cnt[key]
        extra = ((key, prev),) if prev > 0 else ()
        wl = self._waits(qeng, reads, writes, extra)
        self.cnt[key] += 16
        tk = self.cnt[key]
        for b in reads:
            b.r[key] = tk
        for b in writes:
            b.w = (key, tk)
            b.r = {}
        self.q[qeng].append((wl, (lambda e, o=out_ap, a=in_ap: e.dma_start(out=o, in_=a)), key))

    def barrier(self):
        assert not self.pe_pending
        snap = {k: v for k, v in self.cnt.items() if v > 0}
        for e in self.ENG:
            wl = []
            kn = self.known[e]
            for k, v in snap.items():
                if k == e and e == "pe":
                    continue
                if kn.get(k, 0) >= v:
                    continue
                kn[k] = v
                wl.append((k, v))
            if wl:
                self.q[e].append((wl, None, None))

    def replay(self, block):
        m = {"sync": block.sync, "act": block.scalar, "dve": block.vector, "pool": block.gpsimd, "pe": block.tensor}
        for e in self.ENG:
            items = self.q[e]

            def body(engobj, items=items, e=e):
                for wl, fn, sig in items:
                    for k, v in wl:
                        engobj.wait_ge(self.sem[k], v)
                    if fn is None:
                        continue
                    ins = fn(engobj)
                    if sig is True:
                        ins.then_inc(self.sem[e], 1)
                    elif sig is not None:
                        ins.then_inc(self.sem[sig], 16)

            m[e](body)


class Arena:
    def __init__(self, ap):
        self.ap = ap
        self.n = ap.shape[1]
        self.off = 0

    def reset(self, off=0):
        self.off = off

    def f32(self, n):
        n4 = (n + 7) // 8 * 8
        assert self.off + n4 <= self.n, ("arena overflow", self.off, n4, self.n)
        v = self.ap[:, self.off:self.off + n]
        self.off += n4
        return v

    def bf16(self, n):
        n2 = (n + 1) // 2
        return self.f32(n2).bitcast(BF16)[:, 0:n]


def build(seqs=(2048, 8192), phases="ABCDE", debug=False):
    T = sum(seqs)
    NT = T // TT
    seq0 = [sum(seqs[:i]) for i in range(len(seqs))]
    nc = bass.Bass("TRN2", target_bir_lowering=False)
    okind = "ExternalOutput" if debug else "Internal"

    def din(name, shape, dt=F32):
        return nc.dram_tensor(name, list(shape), dt, kind="ExternalInput").ap()

    x_d = din("x", [T, D])
    wgu1_d = din("ffn1_w_gu", [D, 2 * DFF])
    wd1_d = din("ffn1_w_down", [DFF, D])
    wgu2_d = din("ffn2_w_gu", [D, 2 * DFF])
    wd2_d = din("ffn2_w_down", [DFF, D])
    win_d = din("w_in", [D, INW])
    wout_d = din("w_out", [D, D])
    gT_d = din("gT", [128, 3 * KC])
    gpost_d = din("gpost", [4, D])
    convw_d = din("convw_t", [512, CK])
    convv_d = din("convv", [128, 12])
    convb_d = din("conv_b", [512])
    ident_d = din("ident", [128, 128])
    bias_d = din("biasT", [128, 3 * NH * 256], BF16)
    y_d = nc.dram_tensor("y", [T, D], F32, kind="ExternalOutput").ap()
    X1 = nc.dram_tensor("X1", [T, D], F32, kind=okind).ap()
    X2 = nc.dram_tensor("X2", [T, D], F32, kind=okind).ap()
    Qs = nc.dram_tensor("Qs", [T, 512], BF16, kind=okind).ap()
    Ks = nc.dram_tensor("Ks", [T, 512], BF16, kind=okind).ap()
    Vs = nc.dram_tensor("Vs", [T, 512], BF16, kind=okind).ap()
    GluT = nc.dram_tensor("GluT", [512, T], BF16, kind=okind).ap()
    AttnT = nc.dram_tensor("AttnT", [512, T], BF16, kind=okind).ap()

    es = contextlib.ExitStack()
    with es:
        NAR = 52480
        arena_t = es.enter_context(nc.sbuf_tensor("arena", [128, NAR], F32))
        cst_t = es.enter_context(nc.sbuf_tensor("cst", [128, 640], F32))
        banks = [es.enter_context(nc.psum_tensor("bank%d" % i, [128, 512], F32)) for i in range(8)]
        pg = Prog(nc, es)
        AR = Arena(arena_t[:, :])
        CS = Arena(cst_t[:, :])

        identf = Buf(CS.f32(128))
        identb = Buf(CS.bf16(128))
        m05 = Buf(CS.f32(8))
        p05 = Buf(CS.f32(8))
        negM = Buf(CS.f32(8 * len(seqs)))
        onesf = Buf(CS.f32(128))
        pg.dma("sync", identf.ap, ident_d[:, :], writes=[identf])
        pg.op("dve", lambda e: e.tensor_copy(out=identb.ap, in_=identf.ap), reads=[identf], writes=[identb])
        pg.op("pool", lambda e: e.memset(m05.ap, -0.5), writes=[m05])
        pg.op("pool", lambda e: e.memset(p05.ap, 0.5), writes=[p05])
        pg.op("pool", lambda e: e.memset(onesf.ap, 1.0), writes=[onesf])
        pg.op("pool", lambda e: e.memset(negM.ap, 0.0), writes=[negM])

        bankf = [Buf(b[:, :]) for b in banks]

        def bbf(i):
            return banks[i][:, :].bitcast(BF16)

        def rstd_from(ss_buf, rstd_buf, v_buf, scale, eps):
            pg.op("pool", lambda e: e.tensor_scalar(out=v_buf.ap, in0=ss_buf.ap, scalar1=scale, scalar2=eps,
                                                    op0=ALU.mult, op1=ALU.add), reads=[ss_buf], writes=[v_buf])
            pg.op("pool", lambda e: e.tensor_tensor(out=rstd_buf.ap, in0=v_buf.ap, in1=m05.ap[:, 0:1], op=ALU.pow),
                  reads=[v_buf, m05], writes=[rstd_buf])

        def make_gbc(gbc, gT, col0):
            for kc in range(KC):
                pg.op("dve", lambda e, kc=kc: e.tensor_scalar(out=gbc.ap[:, kc * 128:(kc + 1) * 128], in0=onesf.ap,
                                                              scalar1=gT.ap[:, col0 + kc:col0 + kc + 1], scalar2=None,
                                                              op0=ALU.mult), reads=[onesf, gT], writes=[gbc])

        class Front:
            def __init__(self, src, gbc, psT_bank, nxin=2, nxh=4, nhT=1, alt_bank=None):
                self.src = src
                self.gbc = gbc
                self.bank = psT_bank
                self.xin = [Buf(AR.f32(D)) for _ in range(nxin)]
                self.xh = [Buf(AR.bf16(D)) for _ in range(nxh)]
                self.st = [[Buf(CS.f32(1)) for _ in range(3)] for _ in range(nxh)]
                self.hTs = [AR.bf16(KC * TT) for _ in range(nhT)]
                self.hTbs = [[Buf() for _ in range(NSUB)] for _ in range(nhT)]
                self.nhT = nhT
                self.alt_bank = alt_bank
                self.tc = 0
                self.hT = self.hTs[0]
                self.hTb = self.hTbs[0]
                self.nxin = nxin
                self.nxh = nxh
                self.ci = 0
                self.ch = 0
                self.pend = {}
                self.lpend = {}

            def load(self, t, s):
                xin = self.xin[self.ci % self.nxin]
                self.ci += 1
                r0 = t * TT + s * 128
                pg.dma("sync", xin.ap, self.src[r0:r0 + 128, :], writes=[xin])
                self.lpend[(t, s)] = xin

            def stats(self, t, s):
                if (t, s) not in self.lpend:
                    self.load(t, s)
                xin = self.lpend.pop((t, s))
                slot = self.ch % self.nxh
                self.ch += 1
                xh = self.xh[slot]
                ss, v, rs = self.st[slot]
                pg.op("act", lambda e: e.activation(out=xh.ap, in_=xin.ap, func=AF.Square, accum_out=ss.ap),
                      reads=[xin], writes=[xh, ss])
                rstd_from(ss, rs, v, 1.0 / D, RMS_EPS)
                pg.op("dve", lambda e: e.tensor_scalar(out=xh.ap, in0=xin.ap, scalar1=rs.ap, scalar2=None, op0=ALU.mult),
                      reads=[xin, rs], writes=[xh])
                self.pend[(t, s)] = xh

            def transp(self, t, s, bank=None):
                xh = self.pend.pop((t, s))
                if bank is None:
                    bank = self.bank
                    if self.alt_bank is not None and self.tc % 2 == 1:
                        bank = self.alt_bank
                self.tc += 1
                pb = bankf[bank]
                pv = bbf(bank)
                for kc in range(KC):
                    pg.op("pe", lambda e, kc=kc: e.transpose(out=pv[:, kc * 128:(kc + 1) * 128],
                                                             in_=xh.ap[:, kc * 128:(kc + 1) * 128], identity=identb.ap),
                          reads=[xh, identb], writes=[pb], sig=(kc == KC - 1))
                hT_ = self.hTs[t % self.nhT]
                hv = hT_.rearrange("p (k t) -> p k t", t=TT)[:, :, s * 128:(s + 1) * 128]
                pg.op("dve", lambda e: e.tensor_tensor(out=hv, in0=pv.rearrange("p (k t) -> p k t", t=128),
                                                       in1=self.gbc.ap.rearrange("p (k t) -> p k t", t=128), op=ALU.mult),
                      reads=[pb, self.gbc], writes=[self.hTbs[t % self.nhT][s]])

            def hk(self, kc, t=0):
                return self.hTs[t % self.nhT][:, kc * TT:(kc + 1) * TT]

            def sel(self, t):
                self.hT = self.hTs[t % self.nhT]
                self.hTb = self.hTbs[t % self.nhT]

        def load_weights_cast(dst_ap3, src2d, nk, ncols, buf, chunk):
            for k in range(nk):
                for c0 in range(0, ncols, chunk):
                    c1 = min(ncols, c0 + chunk)
                    pg.dma("pool", dst_ap3[:, k * ncols + c0:k * ncols + c1], src2d[k * 128:(k + 1) * 128, c0:c1], writes=[buf])

        def epilogue(psA, psB, res_src, r0, gpost, xr, stt, junk, dst, final_g=None, before=None, preloaded=False):
            ssA, ssB, v, rs, ss3, v3, rs3 = stt
            if not preloaded:
                pg.dma("sync", xr.ap, res_src[r0:r0 + 128, :], writes=[xr])
            if before is not None:
                before()
            jv = junk.ap
            pg.op("act", lambda e: e.activation(out=jv[:, 0:512], in_=psA.ap, func=AF.Square, accum_out=ssA.ap),
                  reads=[psA], writes=[junk, ssA])
            pg.op("act", lambda e: e.activation(out=jv[:, 512:1024], in_=psB.ap, func=AF.Square, accum_out=ssB.ap),
                  reads=[psB], writes=[junk, ssB])
            pg.op("pool", lambda e: e.tensor_tensor(out=ssA.ap, in0=ssA.ap, in1=ssB.ap, op=ALU.add), reads=[ssA, ssB], writes=[ssA])
            rstd_from(ssA, rs, v, 1.0 / D, RMS_EPS)
            for nh, ps in enumerate((psA, psB)):
                sl = slice(nh * 512, (nh + 1) * 512)
                pg.op("dve", lambda e, ps=ps, sl=sl: e.scalar_tensor_tensor(out=ps.ap, in0=ps.ap, scalar=rs.ap, in1=gpost.ap[:, sl],
                                                                           op0=ALU.mult, op1=ALU.mult),
                      reads=[ps, rs, gpost], writes=[ps])
                pg.op("dve", lambda e, ps=ps, sl=sl: e.tensor_tensor(out=xr.ap[:, sl], in0=ps.ap, in1=xr.ap[:, sl], op=ALU.add),
                      reads=[ps, xr], writes=[xr])
            def finish():
                if final_g is not None:
                    pg.op("act", lambda e: e.activation(out=jv, in_=xr.ap, func=AF.Square, accum_out=ss3.ap), reads=[xr], writes=[junk, ss3])
                    rstd_from(ss3, rs3, v3, 1.0 / D, RMS_EPS)
                    pg.op("dve", lambda e: e.scalar_tensor_tensor(out=xr.ap, in0=xr.ap, scalar=rs3.ap, in1=final_g.ap,
                                                                  op0=ALU.mult, op1=ALU.mult), reads=[xr, rs3, final_g], writes=[xr])
                pg.dma("sync", dst[r0:r0 + 128, :], xr.ap, reads=[xr])

            if final_g is None:
                finish()
                return None
            return finish

        def load_gpost(buf, row, scale):
            pg.dma("sync", buf.ap, gpost_d[row, :].partition_broadcast(128), writes=[buf])
            if scale != 1.0:
                pg.op("dve", lambda e: e.tensor_scalar(out=buf.ap, in0=buf.ap, scalar1=scale, scalar2=None, op0=ALU.mult),
                      reads=[buf], writes=[buf])

        def ffn_phase(src, wgu_d, wd_d, gcol, gpost_row, dst, final):
            AR.reset()
            CS.reset(cs_mark)
            Wgu = Buf(AR.bf16(KC * 2 * DFF))
            Wd = Buf(AR.bf16(NFC * D))
            PIECES = (1, 1, 2, 3, 4, 11)
            pstart = [sum(PIECES[:i]) for i in range(len(PIECES))]
            fc2p = []
            for i, n in enumerate(PIECES):
                fc2p += [i] * n
            WguB = [[Buf() for _ in PIECES] for _ in range(2)]
            WdB = [Buf() for _ in range(NFC)]
            Wgu3 = Wgu.ap.rearrange("p (k n) -> p k n", n=2 * DFF)
            wsrc3 = wgu_d.rearrange("(k p) n -> p k n", p=128)
            late = []
            for i, n in enumerate(PIECES):
                for gu in range(2):
                    c0 = gu * DFF + pstart[i] * 128
                    f_ = (lambda c0=c0, n=n, gu=gu, i=i: pg.dma("pool", Wgu3[:, :, c0:c0 + n * 128], wsrc3[:, :, c0:c0 + n * 128],
                                                                 writes=[WguB[gu][i]]))
                    if i == 0:
                        f_()
                    else:
                        late.append(f_)
            Wd3 = Wd.ap.rearrange("p (k n) -> p k n", n=D)
            wdsrc3 = wd_d.rearrange("(k p) n -> p k n", p=128)
            for (k0, k1) in ((0, 2), (2, 6), (6, 14), (14, 22)):
                late.append(lambda k0=k0, k1=k1: pg.dma("pool", Wd3[:, k0:k1, :], wdsrc3[:, k0:k1, :], writes=WdB[k0:k1]))
            gT = Buf(CS.f32(3 * KC))
            pg.dma("sync", gT.ap, gT_d[:, :], writes=[gT])
            gbc = Buf(AR.f32(KC * 128))
            make_gbc(gbc, gT, gcol * KC)
            gpost = Buf(AR.f32(D))
            load_gpost(gpost, gpost_row, 0.5)
            gfin = None
            if final:
                gfin = Buf(AR.f32(D))
                load_gpost(gfin, 3, 1.0)
            fr = Front(src, gbc, 0)
            hid = AR.bf16(NFC * TT)
            hidb = [Buf() for _ in range(NFC)]
            sg = [Buf(AR.f32(TT)) for _ in range(2)]
            xr = [Buf(AR.f32(D)) for _ in range(2)]
            junk = Buf(AR.bf16(D))
            stts = [[Buf(CS.f32(1)) for _ in range(7)] for _ in range(2)]
            guring = [bankf[1], bankf[2], bankf[3]]
            gctr = 0
            psD = [bankf[4], bankf[5], bankf[6], bankf[7]]
            nd = 0
            for s in range(NSUB):
                fr.stats(0, s)
            for f_ in late:
                f_()
            for s in range(NSUB):
                fr.transp(0, s)
            ne = 0
            pend_fin = [None]
            for t in range(NT):
                for fc in range(NFC):
                    if fc == 3 and pend_fin[0] is not None:
                        pend_fin[0]()
                        pend_fin[0] = None
                    if t + 1 < NT and fc in (1, 5, 9, 13):
                        fr.load(t + 1, fc // 4)
                    if t + 1 < NT and fc in (4, 8, 12, 16):
                        fr.stats(t + 1, fc // 4 - 1)
                    g, u = guring[gctr % 3], guring[(gctr + 1) % 3]
                    gctr += 2
                    for kc in range(KC):
                        pg.op("pe", lambda e, kc=kc, fc=fc, g=g: e.matmul(
                            g.ap, lhsT=Wgu.ap[:, kc * 2 * DFF + fc * 128: kc * 2 * DFF + (fc + 1) * 128],
                            rhs=fr.hk(kc), start=(kc == 0), stop=(kc == KC - 1)),
                            reads=[WguB[0][fc2p[fc]]] + fr.hTb, writes=[g], sig=(kc == KC - 1))
                    for kc in range(KC):
                        pg.op("pe", lambda e, kc=kc, fc=fc, u=u: e.matmul(
                            u.ap, lhsT=Wgu.ap[:, kc * 2 * DFF + DFF + fc * 128: kc * 2 * DFF + DFF + (fc + 1) * 128],
                            rhs=fr.hk(kc), start=(kc == 0), stop=(kc == KC - 1)),
                            reads=[WguB[1][fc2p[fc]]] + fr.hTb, writes=[u], sig=(kc == KC - 1))
                    sgb = sg[fc % 2]
                    pg.op("act", lambda e, g=g, sgb=sgb: e.activation(out=sgb.ap, in_=g.ap, func=AF.Silu), reads=[g], writes=[sgb])
                    pg.op("dve", lambda e, u=u, sgb=sgb, fc=fc: e.tensor_tensor(out=hid[:, fc * TT:(fc + 1) * TT], in0=u.ap, in1=sgb.ap,
                                                                                op=ALU.mult), reads=[u, sgb], writes=[hidb[fc]])
                if t + 1 < NT:
                    for s in range(NSUB):
                        if s % 2 == 0:
                            fr.transp(t + 1, s)
                        else:
                            fr.transp(t + 1, s, bank=1 + gctr % 3)
                            gctr += 1
                for s in range(NSUB):
                    pss = []
                    for nh in range(2):
                        pd = psD[nd % 4]
                        nd += 1
                        pss.append(pd)
                        for fc in range(NFC):
                            pg.op("pe", lambda e, fc=fc, s=s, nh=nh, pd=pd: e.matmul(
                                pd.ap, lhsT=hid[:, fc * TT + s * 128: fc * TT + (s + 1) * 128],
                                rhs=Wd.ap[:, fc * D + nh * 512: fc * D + (nh + 1) * 512], start=(fc == 0), stop=(fc == NFC - 1)),
                                reads=[WdB[fc]] + hidb, writes=[pd], sig=(fc == NFC - 1))
                    pend_fin[0] = epilogue(pss[0], pss[1], src, t * TT + s * 128, gpost, xr[ne % 2], stts[ne % 2], junk, dst, gfin,
                                           before=pend_fin[0])
                    ne += 1
            if pend_fin[0] is not None:
                pend_fin[0]()
            pg.barrier()

        def phase_b():
            AR.reset()
            CS.reset(cs_mark)
            Win = Buf(AR.bf16(KC * INW))
            WinB = [Buf() for _ in range(5)]
            Win3 = Win.ap.rearrange("p (k n) -> p k n", n=INW)
            winsrc3 = win_d.rearrange("(k p) n -> p k n", p=128)
            for blk in range(5):
                pg.dma("pool", Win3[:, :, blk * 512:(blk + 1) * 512], winsrc3[:, :, blk * 512:(blk + 1) * 512], writes=[WinB[blk]])
            gT = Buf(CS.f32(3 * KC))
            pg.dma("sync", gT.ap, gT_d[:, :], writes=[gT])
            gbc = Buf(AR.f32(KC * 128))
            make_gbc(gbc, gT, 1 * KC)
            fr = Front(X1, gbc, 0, nhT=2, alt_bank=7)
            qst = [Buf(AR.bf16(512)) for _ in range(2)]
            kst = [Buf(AR.bf16(512)) for _ in range(2)]
            vst = [Buf(AR.bf16(512)) for _ in range(2)]
            sq = [Buf(AR.f32(512)) for _ in range(2)]
            sgm = [Buf(AR.f32(512)) for _ in range(2)]
            glst = [Buf(AR.bf16(4 * TT)) for _ in range(2)]
            nrm = [Buf(CS.f32(8)) for _ in range(4)]
            RM = [Buf(CS.f32(8)), Buf(CS.f32(8))]
            mx = Buf(CS.f32(2))
            m2 = Buf(CS.f32(1))
            Mv = Buf(CS.f32(1))
            dg = Buf(CS.f32(8))
            nm8 = Buf(CS.f32(8))
            ring = [bankf[i] for i in range(1, 7)]
            rc = [0]

            def nxt():
                b = ring[rc[0] % len(ring)]
                rc[0] += 1
                return b

            cq = 0
            for s in range(NSUB):
                fr.stats(0, s)
            for s in range(NSUB):
                fr.transp(0, s)
            for si, S in enumerate(seqs):
                for rm in RM:
                    pg.op("pool", lambda e, rm=rm: e.memset(rm.ap, 0.0), writes=[rm])
                for tl in range(S // TT):
                    t = seq0[si] // TT + tl
                    fr.sel(t)
                    for s in range(NSUB):
                        if t + 1 < NT:
                            fr.load(t + 1, s)
                        r0 = t * TT + s * 128
                        outs = []
                        for blk in range(3):
                            ps = nxt()
                            outs.append(ps)
                            for kc in range(KC):
                                pg.op("pe", lambda e, kc=kc, s=s, blk=blk, ps=ps, hT_=fr.hT: e.matmul(
                                    ps.ap, lhsT=hT_[:, kc * TT + s * 128: kc * TT + (s + 1) * 128],
                                    rhs=Win.ap[:, kc * INW + blk * 512: kc * INW + (blk + 1) * 512],
                                    start=(kc == 0), stop=(kc == KC - 1)), reads=[WinB[blk], fr.hTb[s]], writes=[ps], sig=(kc == KC - 1))
                        for wi, (ps, stg, scale, dstT) in enumerate(((outs[0], qst[cq % 2], 0.125, Qs), (outs[1], kst[cq % 2], 1.0, Ks))):
                            sqb = sq[wi]
                            nb = nrm[(cq % 2) * 2 + wi]
                            pg.op("act", lambda e, ps=ps, stg=stg, scale=scale: e.activation(out=stg.ap, in_=ps.ap, func=AF.Copy, scale=scale),
                                  reads=[ps], writes=[stg])
                            pg.op("act", lambda e, stg=stg, sqb=sqb: e.activation(out=sqb.ap, in_=stg.ap, func=AF.Square), reads=[stg], writes=[sqb])
                            pg.op("dve", lambda e, sqb=sqb, nb=nb: e.tensor_reduce(out=nb.ap, in_=sqb.ap.rearrange("p (h d) -> p h d", d=64),
                                                                                   axis=AX.X, op=ALU.add), reads=[sqb], writes=[nb])
                            pg.op("dve", lambda e, nb=nb, wi=wi: e.tensor_tensor(out=RM[wi].ap, in0=RM[wi].ap, in1=nb.ap, op=ALU.max),
                                  reads=[nb, RM[wi]], writes=[RM[wi]])
                            pg.dma("sync", dstT[r0:r0 + 128, :], stg.ap, reads=[stg])
                        vb = vst[cq % 2]
                        pg.op("dve", lambda e, vb=vb, ps=outs[2]: e.tensor_copy(out=vb.ap, in_=ps.ap), reads=[outs[2]], writes=[vb])
                        pg.dma("sync", Vs[r0:r0 + 128, :], vb.ap, reads=[vb])
                        cq += 1
                        if t + 1 < NT and s >= 1:
                            fr.stats(t + 1, s - 1)
                    gl = glst[tl % 2]
                    for c in range(4):
                        if c == 1 and t + 1 < NT:
                            fr.stats(t + 1, 3)
                        if c == 3 and t + 1 < NT:
                            for s in range(NSUB):
                                fr.transp(t + 1, s)
                        pa, pb = nxt(), nxt()
                        for (ps, col) in ((pa, 1536 + c * 128), (pb, 2048 + c * 128)):
                            for kc in range(KC):
                                pg.op("pe", lambda e, kc=kc, ps=ps, col=col, t=t: e.matmul(
                                    ps.ap, lhsT=Win.ap[:, kc * INW + col: kc * INW + col + 128], rhs=fr.hk(kc, t),
                                    start=(kc == 0), stop=(kc == KC - 1)), reads=[WinB[col // 512]] + fr.hTb, writes=[ps], sig=(kc == KC - 1))
                        sb_ = sgm[c % 2]
                        pg.op("act", lambda e, pb=pb, sb_=sb_: e.activation(out=sb_.ap, in_=pb.ap, func=AF.Sigmoid), reads=[pb], writes=[sb_])
                        pg.op("dve", lambda e, pa=pa, sb_=sb_, gl=gl, c=c: e.tensor_tensor(out=gl.ap[:, c * TT:(c + 1) * TT], in0=pa.ap, in1=sb_.ap,
                                                                                          op=ALU.mult), reads=[pa, sb_], writes=[gl])
                    pg.dma("sync", GluT.rearrange("(c p) t -> p c t", p=128)[:, :, t * TT:(t + 1) * TT],
                           gl.ap.rearrange("p (c t) -> p c t", t=TT), reads=[gl])
                px = nxt()
                for wi in range(2):
                    pg.op("pe", lambda e, wi=wi, px=px: e.transpose(out=px.ap[0:8, wi * 128:(wi + 1) * 128], in_=RM[wi].ap, identity=identf.ap),
                          reads=[RM[wi], identf], writes=[px], sig=True)
                pg.op("dve", lambda e, px=px: e.tensor_reduce(out=mx.ap[0:8, :], in_=px.ap[0:8, 0:256].rearrange("p (a b) -> p a b", b=128),
                                                              axis=AX.X, op=ALU.max), reads=[px], writes=[mx])
                pg.op("dve", lambda e: e.tensor_tensor(out=m2.ap[0:8, :], in0=mx.ap[0:8, 0:1], in1=mx.ap[0:8, 1:2], op=ALU.mult),
                      reads=[mx], writes=[m2])
                pg.op("pool", lambda e: e.tensor_tensor(out=Mv.ap[0:8, :], in0=m2.ap[0:8, :], in1=p05.ap[0:8, 0:1], op=ALU.pow),
                      reads=[m2, p05], writes=[Mv])
                pg.op("dve", lambda e: e.tensor_scalar(out=dg.ap[0:8, :], in0=identf.ap[0:8, 0:8], scalar1=Mv.ap[0:8, :], scalar2=-1.001,
                                                       op0=ALU.mult, op1=ALU.mult), reads=[identf, Mv], writes=[dg])
                px2 = nxt()
                pg.op("pe", lambda e, px2=px2: e.matmul(px2.ap[:, 0:8], lhsT=onesf.ap[0:8, :], rhs=dg.ap[0:8, :], start=True, stop=True),
                      reads=[onesf, dg], writes=[px2], sig=True)
                pg.op("dve", lambda e, px2=px2: e.tensor_copy(out=nm8.ap, in_=px2.ap[:, 0:8]), reads=[px2], writes=[nm8])
                pg.op("dve", lambda e, si=si: e.tensor_tensor(out=negM.ap[:, si * 4:(si + 1) * 4], in0=nm8.ap[:, 0:8:2], in1=nm8.ap[:, 1:8:2],
                                                              op=ALU.min), reads=[nm8], writes=[negM])
            pg.barrier()

        def phase_c():
            AR.reset()
            CS.reset(cs_mark)
            Accs = [AR.f32(NH * SB) for _ in range(2)]
            Acc3s = [a_.rearrange("p (h t) -> p h t", t=SB) for a_ in Accs]
            accbs = [[Buf() for _ in range(4)] for _ in range(2)]
            biasb = Buf(AR.bf16(3 * NH * 256))
            pg.dma("sync", biasb.ap, bias_d[:, :], writes=[biasb])
            aouts = [Buf(AR.bf16(SB)) for _ in range(2)]
            Rbs = [Buf(AR.f32(SB)) for _ in range(2)]
            qland = [Buf(AR.bf16(512)) for _ in range(2)]
            kland = [Buf(AR.bf16(512)) for _ in range(4)]
            vland = [Buf(AR.bf16(512)) for _ in range(4)]
            qT = [Buf(AR.bf16(1024)) for _ in range(2)]
            pT = [Buf(AR.bf16(512)) for _ in range(3)]

            class Slot:
                def __init__(self):
                    self.kT = Buf(AR.bf16(512))
                    self.va = Buf(AR.bf16(1024))

            slots = {"N": [Slot() for _ in range(4)], "F": [Slot() for _ in range(2)], "L": [Slot() for _ in range(2)]}
            sctr = {"N": 0, "F": 0, "L": 0}
            for b in qT + kland + qland + vland:
                pg.op("pool", lambda e, b=b: e.memset(b.ap, 0.0), writes=[b])
            for sl in slots["N"]:
                pg.op("pool", lambda e, sl=sl: e.memset(sl.va.ap, 1.0), writes=[sl.va])
            for typ, (vlo, vhi) in (("F", (64, 128)), ("L", (0, 64))):
                for sl in slots[typ]:
                    pg.op("pool", lambda e, sl=sl: e.memset(sl.va.ap, 0.0), writes=[sl.va])
                    pg.op("pool", lambda e, sl=sl: e.memset(sl.kT.ap, 0.0), writes=[sl.kT])
                    v3 = sl.va.ap.rearrange("p (h d) -> p h d", d=128)
                    pg.op("dve", lambda e, v3=v3, vlo=vlo, vhi=vhi: e.memset(v3[vlo:vhi, 0:8:2, 64:128], 1.0), writes=[sl.va])
                    pg.op("dve", lambda e, v3=v3, vlo=vlo, vhi=vhi: e.memset(v3[vlo:vhi, 1:8:2, 0:64], 1.0), writes=[sl.va])
            psT = [0, 1]
            psS = [bankf[2], bankf[3], bankf[4]]
            psO = [bankf[5], bankf[6], bankf[7]]
            tctr = [0]

            tiles = []
            keys = []
            sbg = -1
            for si, S in enumerate(seqs):
                for sb in range(S // SB):
                    b0 = sb * SB
                    sbg += 1
                    for di, d in enumerate(DILS):
                        L = S // d
                        nt = SB // d // 128
                        for r in range(d):
                            base = len(keys)
                            for j in range(nt + 1):
                                Uk = b0 // d + 128 * j - 64
                                lo = max(0, -Uk)
                                hi = min(128, L - Uk)
                                typ = "F" if lo > 0 else ("L" if hi < 128 else "N")
                                keys.append(dict(tk=seq0[si] + r + d * (Uk + lo), lo=lo, hi=hi, typ=typ, d=d))
                            for i in range(nt):
                                tiles.append(dict(si=si, sbk=(si, sb), sbg=sbg, di=di, d=d, r=r, i=i, k0=base + i, k1=base + i + 1,
                                                  tq=seq0[si] + b0 + r + d * 128 * i, tb=seq0[si] + b0))
            NTL = len(tiles)
            lctr = [0]

            def load_key(m):
                k = keys[m]
                if "land" in k:
                    return
                li = lctr[0] % 4
                lctr[0] += 1
                k["land"] = li
                kl, vl = kland[li], vland[li]
                lo, hi, d = k["lo"], k["hi"], k["d"]
                n = hi - lo
                if n < 128:
                    zlo = 0 if lo > 0 else hi
                    pg.op("pool", lambda e: e.memset(kl.ap[zlo:zlo + 64, :], 0.0), writes=[kl])
                pg.dma("sync", kl.ap[lo:hi, :], Ks[sl_(k["tk"], n, d), :], writes=[kl])
                pg.dma("sync", vl.ap[lo:hi, :], Vs[sl_(k["tk"], n, d), :], writes=[vl])

            def tr_key(m):
                k = keys[m]
                if "slot" in k:
                    return
                typ = k["typ"]
                sl = slots[typ][sctr[typ] % len(slots[typ])]
                sctr[typ] += 1
                k["slot"] = sl
                kl, vl = kland[k["land"]], vland[k["land"]]
                bi = psT[tctr[0] % 2]
                tctr[0] += 1
                pb, pv = bankf[bi], bbf(bi)
                for c in range(4):
                    pg.op("pe", lambda e, c=c: e.transpose(out=pv[:, c * 128:(c + 1) * 128], in_=kl.ap[:, c * 128:(c + 1) * 128],
                                                           identity=identb.ap), reads=[kl, identb], writes=[pb], sig=(c == 3))
                if tctr[0] % 4 < 2:
                    pg.op("act", lambda e: e.activation(out=sl.kT.ap, in_=pv[:, 0:512], func=AF.Copy), reads=[pb], writes=[sl.kT])
                else:
                    pg.op("dve", lambda e: e.tensor_copy(out=sl.kT.ap, in_=pv[:, 0:512]), reads=[pb], writes=[sl.kT])
                lo, hi = k["lo"], k["hi"]
                v3 = sl.va.ap.rearrange("p (h d) -> p h d", d=128)
                l3 = vl.ap.rearrange("p (h d) -> p h d", d=64)
                if hi - lo == 128:
                    pg.op("pool", lambda e: e.tensor_copy(out=v3[:, 0:8:2, 0:64], in_=l3[:, 0:8:2, :]), reads=[vl], writes=[sl.va])
                    pg.op("pool", lambda e: e.tensor_copy(out=v3[:, 1:8:2, 64:128], in_=l3[:, 1:8:2, :]), reads=[vl], writes=[sl.va])
                else:
                    pg.op("act", lambda e: e.activation(out=v3[lo:hi, 0:8:2, 0:64], in_=l3[lo:hi, 0:8:2, :], func=AF.Copy), reads=[vl], writes=[sl.va])
                    pg.op("act", lambda e: e.activation(out=v3[lo:hi, 1:8:2, 64:128], in_=l3[lo:hi, 1:8:2, :], func=AF.Copy), reads=[vl], writes=[sl.va])

            def load_tile(n):
                tl = tiles[n]
                qb = qland[n % 2]
                pg.dma("sync", qb.ap, Qs[sl_(tl["tq"], 128, tl["d"]), :], writes=[qb])
                load_key(tl["k0"])
                load_key(tl["k1"])

            def tr_tile(n):
                tl = tiles[n]
                qb = qland[n % 2]
                qtb = qT[n % 2]
                bi = psT[tctr[0] % 2]
                tctr[0] += 1
                pb, pv = bankf[bi], bbf(bi)
                for c in range(4):
                    pg.op("pe", lambda e, c=c: e.transpose(out=pv[:, c * 128:(c + 1) * 128], in_=qb.ap[:, c * 128:(c + 1) * 128],
                                                           identity=identb.ap), reads=[qb, identb], writes=[pb], sig=(c == 3))
                for hh in range(2):
                    pg.op("dve", lambda e, hh=hh: e.tensor_copy(
                        out=qtb.ap[64 * hh:64 * hh + 64, :].rearrange("p (c v q) -> p c v q", v=2, q=128)[:, :, hh, :],
                        in_=pv[64 * hh:64 * hh + 64, 0:512].rearrange("p (c q) -> p c q", q=128)), reads=[pb], writes=[qtb])
                tr_key(tl["k0"])
                tr_key(tl["k1"])

            P = [(n, hp) for n in range(NTL) for hp in range(4)]

            def emit_S(idx):
                n, hp = P[idx]
                tl = tiles[n]
                qtb = qT[n % 2]
                ks = (keys[tl["k0"]]["slot"], keys[tl["k1"]]["slot"])
                ps, pt = psS[idx % 3], pT[idx % 3]
                di, si = tl["di"], tl["si"]
                for hh in range(2):
                    h = 2 * hp + hh
                    bo = (di * NH + h) * 256
                    pg.op("pe", lambda e, bo=bo, hh=hh: e.matmul(ps.ap[:, hh * 256:(hh + 1) * 256], lhsT=identb.ap,
                                                                 rhs=biasb.ap[:, bo:bo + 256], start=True, stop=False),
                          reads=[identb, biasb], writes=[ps], sig=False)
                    for kt in range(2):
                        pg.op("pe", lambda e, kt=kt, hh=hh: e.matmul(
                            ps.ap[:, hh * 256 + kt * 128: hh * 256 + (kt + 1) * 128],
                            lhsT=ks[kt].kT.ap[:, hp * 128:(hp + 1) * 128],
                            rhs=qtb.ap[:, (hp * 2 + hh) * 128:(hp * 2 + hh + 1) * 128], start=False, stop=(kt == 1)),
                            reads=[ks[kt].kT, qtb], writes=[ps], sig=(kt == 1 and hh == 1))
                pg.op("act", lambda e: e.activation(out=pt.ap, in_=ps.ap, func=AF.Exp, bias=negM.ap[:, si * 4 + hp: si * 4 + hp + 1]),
                      reads=[ps, negM], writes=[pt])

            def emit_PV(idx):
                n, hp = P[idx]
                tl = tiles[n]
                ks = (keys[tl["k0"]]["slot"], keys[tl["k1"]]["slot"])
                pt, po = pT[idx % 3], psO[idx % 3]
                d, r, i, di = tl["d"], tl["r"], tl["i"], tl["di"]
                for hh in range(2):
                    h = 2 * hp + hh
                    for kt in range(2):
                        pg.op("pe", lambda e, kt=kt, hh=hh, h=h: e.matmul(
                            po.ap[:, hh * 128:(hh + 1) * 128], lhsT=ks[kt].va.ap[:, h * 128:(h + 1) * 128],
                            rhs=pt.ap[:, hh * 256 + kt * 128: hh * 256 + (kt + 1) * 128], start=(kt == 0), stop=(kt == 1)),
                            reads=[ks[kt].va, pt], writes=[po], sig=(kt == 1 and hh == 1))
                Acc3 = Acc3s[tl["sbg"] % 2]
                accb = accbs[tl["sbg"] % 2]
                av = Acc3[:, 2 * hp:2 * hp + 2, sl_(r + d * 128 * i, 128, d)]
                pv_ = po.ap[:, 0:256].rearrange("p (h t) -> p h t", t=128)
                if di == 0:
                    pg.op("dve", lambda e: e.tensor_copy(out=av, in_=pv_), reads=[po], writes=[accb[hp]])
                else:
                    pg.op("dve", lambda e: e.tensor_tensor(out=av, in0=pv_, in1=av, op=ALU.add), reads=[po, accb[hp]], writes=[accb[hp]])

            def norm_head(tl, h):
                Acc = Accs[tl["sbg"] % 2]
                accb = accbs[tl["sbg"] % 2]
                c = h // 2
                nlo, dlo = (0, 64) if h % 2 == 0 else (64, 0)
                Av = Acc[:, h * SB:(h + 1) * SB]
                Rb = Rbs[h % 2]
                ao = aouts[c % 2]
                pg.op("act", lambda e: e.activation(out=Rb.ap[nlo:nlo + 64, :], in_=Av[dlo:dlo + 64, :], func=AF.Ln),
                      reads=[accb[c]], writes=[Rb])
                pg.op("act", lambda e: e.activation(out=Rb.ap[nlo:nlo + 64, :], in_=Rb.ap[nlo:nlo + 64, :], func=AF.Exp, scale=-1.0),
                      reads=[Rb], writes=[Rb])
                pg.op("dve", lambda e: e.tensor_tensor(out=ao.ap[nlo:nlo + 64, :], in0=Av[nlo:nlo + 64, :], in1=Rb.ap[nlo:nlo + 64, :], op=ALU.mult),
                      reads=[accb[c], Rb], writes=[ao])
                if h % 2 == 1:
                    tb = tl["tb"]
                    pg.dma("sync", AttnT[c * 128:(c + 1) * 128, tb:tb + SB], ao.ap, reads=[ao])

            pend_norm = []
            load_tile(0)
            if NTL > 1:
                load_tile(1)
            tr_tile(0)
            emit_S(0)
            emit_S(1)
            for idx, (n, hp) in enumerate(P):
                if hp == 0:
                    if n + 1 < NTL:
                        tr_tile(n + 1)
                    if n + 2 < NTL:
                        load_tile(n + 2)
                if idx + 2 < len(P):
                    emit_S(idx + 2)
                emit_PV(idx)
                if hp == 3:
                    if pend_norm:
                        norm_head(*pend_norm.pop(0))
                    if n + 1 == NTL or tiles[n + 1]["sbk"] != tiles[n]["sbk"]:
                        while pend_norm:
                            norm_head(*pend_norm.pop(0))
                        pend_norm.extend((tiles[n], h) for h in range(NH))
            while pend_norm:
                norm_head(*pend_norm.pop(0))
            pg.barrier()

        def phase_d():
            AR.reset()
            CS.reset(cs_mark)
            Wout = Buf(AR.bf16(KC * D))
            cwT = Buf(AR.f32(4 * CK))
            pg.dma("sync", cwT.ap.rearrange("p (c j) -> p c j", j=CK), convw_d.rearrange("(c p) j -> p c j", p=128), writes=[cwT])
            diag = Buf(AR.bf16(CK * 4 * 128))
            diagb = [Buf() for _ in range(CK * 4)]
            for j in range(CK):
                for c in range(4):
                    o = (j * 4 + c) * 128
                    if (j * 4 + c) % 2 == 0:
                        pg.op("dve", lambda e, o=o, c=c, j=j: e.tensor_scalar(out=diag.ap[:, o:o + 128], in0=identf.ap,
                                                                              scalar1=cwT.ap[:, c * CK + j: c * CK + j + 1], scalar2=None,
                                                                              op0=ALU.mult), reads=[identf, cwT], writes=[diagb[j * 4 + c]])
                    else:
                        pg.op("act", lambda e, o=o, c=c, j=j: e.activation(out=diag.ap[:, o:o + 128], in_=identf.ap, func=AF.Copy,
                                                                           scale=cwT.ap[:, c * CK + j: c * CK + j + 1]),
                              reads=[identf, cwT], writes=[diagb[j * 4 + c]])
            cv = Buf(CS.f32(12))
            pg.dma("sync", cv.ap, convv_d[:, :], writes=[cv])
            ones2 = Buf(CS.f32(2))
            pg.op("pool", lambda e: e.memset(ones2.ap, 1.0), writes=[ones2])
            gpost = Buf(AR.f32(D))
            load_gpost(gpost, 1, 1.0)
            WN = TT + CK - 1
            gw = [Buf(AR.bf16(4 * WN + 2)) for _ in range(3)]
            at = [Buf(AR.bf16(4 * TT)) for _ in range(4)]
            loaded_d = set()
            cT = [AR.f32(4 * TT) for _ in range(2)]
            cTb = [[Buf() for _ in range(4)] for _ in range(2)]
            csq = [AR.f32(4 * TT) for _ in range(2)]
            csqb = [[Buf() for _ in range(4)] for _ in range(2)]
            convT = [AR.bf16(4 * TT) for _ in range(2)]
            convTb = [[Buf() for _ in range(4)] for _ in range(2)]
            dR = [Buf(AR.f32(128)) for _ in range(2)]
            dS = [Buf(AR.f32(128)) for _ in range(2)]
            sm = [[Buf(CS.f32(4)) for _ in range(6)] for _ in range(2)]
            xr = [Buf(AR.f32(D)) for _ in range(3)]
            junk = Buf(AR.bf16(D))
            stts = [[Buf(CS.f32(1)) for _ in range(7)] for _ in range(2)]
            xr_issued = set()

            def xr_load(e):
                if e in xr_issued or e >= NTD * NSUB:
                    return
                xr_issued.add(e)
                si_, tl_ = tl_list[e // NSUB]
                r0_ = seq0[si_] + tl_ * TT + (e % NSUB) * 128
                pg.dma("sync", xr[e % 3].ap, X1[r0_:r0_ + 128, :], writes=[xr[e % 3]])

            psC = [bankf[0], bankf[1]]
            bcR, bcS = bankf[2], bankf[3]
            pst = bcS
            psO = [bankf[4], bankf[5], bankf[6], bankf[7]]
            tl_list = []
            for si, S in enumerate(seqs):
                for tl in range(S // TT):
                    tl_list.append((si, tl))
            NTD = len(tl_list)
            GT = GluT.rearrange("(c p) t -> p c t", p=128)
            AT3 = AttnT.rearrange("(c p) t -> p c t", p=128)
            cc = [0]
            oc = [0]
            ec = [0]

            def dload(t):
                if t in loaded_d or t >= NTD:
                    return
                loaded_d.add(t)
                si, tl = tl_list[t]
                S = seqs[si]
                t0 = seq0[si] + tl * TT
                g = gw[t % 3]
                gv = g.ap[:, 0:4 * WN].rearrange("p (c w) -> p c w", w=WN)
                lo, hi = 0, WN
                if tl == 0:
                    lo = 15
                    pg.op("pool", lambda e: e.memset(gv[:, :, 0:15], 0.0), writes=[g])
                if tl == S // TT - 1:
                    hi = WN - 15
                    pg.op("pool", lambda e: e.memset(gv[:, :, WN - 15:WN], 0.0), writes=[g])
                pg.dma("sync", gv[:, :, lo:hi], GT[:, :, t0 - 15 + lo:t0 - 15 + hi], writes=[g])
                a = at[t % 4]
                pg.dma("sync", a.ap.rearrange("p (c t) -> p c t", t=TT), AT3[:, :, t0:t0 + TT], writes=[a])

            def conv(t):
                dload(t)
                dload(t + 1)
                g = gw[t % 3]
                k = t % 2
                for c in range(4):
                    pc = psC[cc[0] % 2]
                    cc[0] += 1
                    for j in range(CK):
                        o = (j * 4 + c) * 128
                        pg.op("pe", lambda e, c=c, j=j, o=o, pc=pc: e.matmul(
                            pc.ap, lhsT=diag.ap[:, o:o + 128], rhs=g.ap[:, c * WN + j: c * WN + j + TT],
                            start=(j == 0), stop=(j == CK - 1)), reads=[g, diagb[j * 4 + c]], writes=[pc], sig=(j == CK - 1))
                    pg.op("act", lambda e, c=c, pc=pc: e.activation(out=cT[k][:, c * TT:(c + 1) * TT], in_=pc.ap, func=AF.Identity,
                                                                    bias=cv.ap[:, 8 + c:9 + c]), reads=[pc, cv], writes=[cTb[k][c]])
                    pg.op("act", lambda e, c=c, pc=pc: e.activation(out=csq[k][:, c * TT:(c + 1) * TT], in_=pc.ap, func=AF.Square,
                                                                    bias=cv.ap[:, 8 + c:9 + c]), reads=[pc, cv], writes=[csqb[k][c]])

            def stats(t):
                k = t % 2
                for s in range(NSUB):
                    for gi, (src, srcb) in enumerate(((cT[k], cTb[k]), (csq[k], csqb[k]))):
                        for c in range(4):
                            pg.op("pe", lambda e, c=c, s=s, gi=gi, src=src: e.matmul(
                                pst.ap[:, gi * 8 + s * 2: gi * 8 + s * 2 + 2], lhsT=src[:, c * TT + s * 128: c * TT + (s + 1) * 128],
                                rhs=ones2.ap, start=(c == 0), stop=(c == 3)), reads=[srcb[c], ones2], writes=[pst],
                                sig=(c == 3 and gi == 1 and s == NSUB - 1))
                mean, msq, var, vv, rstd, shift = sm[k]
                pg.op("dve", lambda e: e.tensor_scalar(out=mean.ap, in0=pst.ap[:, 0:8:2], scalar1=1.0 / 512, scalar2=None, op0=ALU.mult),
                      reads=[pst], writes=[mean])
                pg.op("dve", lambda e: e.tensor_tensor(out=msq.ap, in0=mean.ap, in1=mean.ap, op=ALU.mult), reads=[mean], writes=[msq])
                pg.op("dve", lambda e: e.scalar_tensor_tensor(out=var.ap, in0=pst.ap[:, 8:16:2], scalar=1.0 / 512, in1=msq.ap,
                                                              op0=ALU.mult, op1=ALU.subtract), reads=[pst, msq], writes=[var])
                pg.op("pool", lambda e: e.tensor_scalar(out=vv.ap, in0=var.ap, scalar1=1.0, scalar2=LN_EPS, op0=ALU.mult, op1=ALU.add),
                      reads=[var], writes=[vv])
                pg.op("pool", lambda e: e.tensor_tensor(out=rstd.ap, in0=vv.ap, in1=m05.ap[:, 0:4], op=ALU.pow), reads=[vv, m05], writes=[rstd])
                pg.op("dve", lambda e: e.scalar_tensor_tensor(out=shift.ap, in0=mean.ap, scalar=-1.0, in1=rstd.ap, op0=ALU.mult, op1=ALU.mult),
                      reads=[mean, rstd], writes=[shift])

            def bc_norm(t):
                k = t % 2
                mean, msq, var, vv, rstd, shift = sm[k]
                for s in range(NSUB):
                    r_, s_ = dR[s % 2], dS[s % 2]
                    pg.op("dve", lambda e, s=s, r_=r_: e.tensor_scalar(out=r_.ap, in0=identf.ap, scalar1=rstd.ap[:, s:s + 1], scalar2=None, op0=ALU.mult),
                          reads=[identf, rstd], writes=[r_])
                    pg.op("dve", lambda e, s=s, s_=s_: e.tensor_scalar(out=s_.ap, in0=identf.ap, scalar1=shift.ap[:, s:s + 1], scalar2=None, op0=ALU.mult),
                          reads=[identf, shift], writes=[s_])
                    pg.op("pe", lambda e, s=s, r_=r_: e.matmul(bcR.ap[:, s * 128:(s + 1) * 128], lhsT=onesf.ap, rhs=r_.ap, start=True, stop=True),
                          reads=[onesf, r_], writes=[bcR], sig=True)
                    pg.op("pe", lambda e, s=s, s_=s_: e.matmul(bcS.ap[:, s * 128:(s + 1) * 128], lhsT=onesf.ap, rhs=s_.ap, start=True, stop=True),
                          reads=[onesf, s_], writes=[bcS], sig=True)
                for c in range(4):
                    cv_ = cT[k][:, c * TT:(c + 1) * TT]
                    pg.op("dve", lambda e, cv_=cv_: e.tensor_tensor(out=cv_, in0=cv_, in1=bcR.ap, op=ALU.mult), reads=[bcR, cTb[k][c]], writes=[cTb[k][c]])
                    pg.op("dve", lambda e, cv_=cv_: e.tensor_tensor(out=cv_, in0=cv_, in1=bcS.ap, op=ALU.add), reads=[bcS, cTb[k][c]], writes=[cTb[k][c]])
                    pg.op("act", lambda e, cv_=cv_, c=c: e.activation(out=convT[k][:, c * TT:(c + 1) * TT], in_=cv_, func=AF.Silu,
                                                                      scale=cv.ap[:, c:c + 1], bias=cv.ap[:, 4 + c:5 + c]),
                          reads=[cTb[k][c], cv], writes=[convTb[k][c]])

            def outproj(t):
                si, tl = tl_list[t]
                k = t % 2
                a = at[t % 4]
                for s in range(NSUB):
                    pss = []
                    for nh in range(2):
                        po = psO[oc[0] % 4]
                        oc[0] += 1
                        pss.append(po)
                        for kc in range(KC):
                            src = a.ap if kc < 4 else convT[k]
                            kk = kc % 4
                            pg.op("pe", lambda e, kc=kc, kk=kk, src=src, nh=nh, po=po, s=s: e.matmul(
                                po.ap, lhsT=src[:, kk * TT + s * 128: kk * TT + (s + 1) * 128],
                                rhs=Wout.ap[:, kc * D + nh * 512: kc * D + (nh + 1) * 512], start=(kc == 0), stop=(kc == KC - 1)),
                                reads=[a, Wout] + convTb[k], writes=[po], sig=(kc == KC - 1))
                    r0 = seq0[si] + tl * TT + s * 128
                    xr_load(ec[0])
                    xr_load(ec[0] + 1)
                    epilogue(pss[0], pss[1], X1, r0, gpost, xr[ec[0] % 3], stts[ec[0] % 2], junk, X2, None, preloaded=True)
                    ec[0] += 1

            dload(0)
            dload(1)
            pg.dma("pool", Wout.ap.rearrange("p (k n) -> p k n", n=D), wout_d.rearrange("(k p) n -> p k n", p=128), writes=[Wout])
            conv(0)
            for t in range(NTD + 1):
                if t < NTD:
                    stats(t)
                if t + 1 < NTD:
                    conv(t + 1)
                if t < NTD:
                    bc_norm(t)
                if t >= 1:
                    outproj(t - 1)
            pg.barrier()

        cs_mark = CS.off
        if "A" in phases:
            ffn_phase(x_d, wgu1_d, wd1_d, 0, 0, X1, False)
        if "B" in phases:
            phase_b()
        if "C" in phases:
            phase_c()
        if "D" in phases:
            phase_d()
        if "E" in phases:
            ffn_phase(X2, wgu2_d, wd2_d, 2, 2, y_d, True)
        pg.barrier()
        with nc.Block() as block:
            pg.replay(block)
    return nc


def host_consts():
    ident = np.eye(128, dtype=np.float32)
    p = np.arange(128)[:, None, None]
    kt = np.arange(2)[None, :, None]
    q = np.arange(128)[None, None, :]
    rel = (p - 64 + 128 * kt) - q
    arel = np.abs(rel).astype(np.float32)
    slopes = 2.0 ** (-8.0 * np.arange(1, NH + 1, dtype=np.float32) / NH)
    tab = np.zeros((128, 3, NH, 2, 128), np.float32)
    for di, d in enumerate(DILS):
        for h in range(NH):
            b = -(slopes[h] * d) * arel
            tab[:, di, h] = np.where(arel <= 64, b, MASKV)
    return ident, tab.reshape(128, 3 * NH * 256).astype(ml_dtypes.bfloat16)


def make_common(ffn1_pre_g, ffn1_w_gu, ffn1_w_down, ffn1_post_g, mix_pre_g, w_in, conv_w, conv_b, conv_ln_g, conv_ln_b,
                w_out, mix_post_g, ffn2_pre_g, ffn2_w_gu, ffn2_w_down, ffn2_post_g, final_g):
    f = lambda a: np.ascontiguousarray(np.asarray(a, dtype=np.float32))
    ident, biasT = host_consts()
    gT = np.concatenate([f(g).reshape(KC, 128).T for g in (ffn1_pre_g, mix_pre_g, ffn2_pre_g)], axis=1)
    gpost = np.stack([f(g).reshape(D) for g in (ffn1_post_g, mix_post_g, ffn2_post_g, final_g)], axis=0)
    convv = np.concatenate([f(conv_ln_g).reshape(4, 128).T, f(conv_ln_b).reshape(4, 128).T, f(conv_b).reshape(4, 128).T], axis=1)
    return {
        "ffn1_w_gu": f(ffn1_w_gu).reshape(D, 2 * DFF), "ffn1_w_down": f(ffn1_w_down).reshape(DFF, D),
        "ffn2_w_gu": f(ffn2_w_gu).reshape(D, 2 * DFF), "ffn2_w_down": f(ffn2_w_down).reshape(DFF, D),
        "w_in": f(w_in).reshape(D, INW), "w_out": f(w_out).reshape(D, D),
        "gT": f(gT), "gpost": f(gpost), "convw_t": f(f(conv_w).reshape(CK, 512).T), "convv": f(convv),
        "conv_b": f(conv_b).reshape(512), "ident": ident, "biasT": biasT,
    }


_NC_CACHE = {}


def kernel(x_prompt, x_sample, **w):
    xp = np.asarray(x_prompt, dtype=np.float32)
    xs = np.asarray(x_sample, dtype=np.float32)
    n = 8
    common = make_common(**w)
    if "nc" not in _NC_CACHE:
        _NC_CACHE["nc"] = build()
    nc = _NC_CACHE["nc"]
    in_maps = []
    for i in range(n):
        m = dict(common)
        m["x"] = np.concatenate([xp[i], xs[i]], axis=0)
        in_maps.append(m)
    res = run_bass_kernel_spmd(nc, in_maps, core_ids=list(range(n)))
    ys = [np.asarray(r["y"]) for r in res.results]
    y_prompt = np.stack([y[:2048] for y in ys], axis=0).astype(np.float32)
    y_sample = np.stack([y[2048:] for y in ys], axis=0).astype(np.float32)
    return (y_prompt, y_sample)
```

```python
import contextlib
import numpy as np
import ml_dtypes
import concourse.bass as bass
import concourse.mybir as mybir
from concourse.bass_utils import run_bass_kernel_spmd

F32 = mybir.dt.float32
BF16 = mybir.dt.bfloat16
ALU = mybir.AluOpType
AF = mybir.ActivationFunctionType
AX = mybir.AxisListType

D = 1024
DFF = 2816
NFC = DFF // 128
KC = 8
TT = 512
NSUB = 4
NH = 8
INW = 2560
CK = 31
RMS_EPS = 1e-6
LN_EPS = 1e-5
DILS = (1, 4, 16)
SB = 2048
MASKV = -30000.0


def sl_(start, n, d):
    return slice(start, start + (n - 1) * d + 1, d)


class Buf:
    __slots__ = ("ap", "w", "r")

    def __init__(self, ap=None):
        self.ap = ap
        self.w = None
        self.r = {}


class Prog:
    ENG = ("sync", "act", "dve", "pool", "pe")

    def __init__(self, nc, es, nds=32):
        self.nc = nc
        self.q = {e: [] for e in self.ENG}
        self.sem = {}
        self.cnt = {}
        for e in ("act", "dve", "pool", "pe"):
            self.sem[e] = es.enter_context(nc.semaphore("s_" + e))
            self.cnt[e] = 0
        self.NDS = nds
        self.NDP = 8
        for i in range(nds):
            k = "d%d" % i
            self.sem[k] = es.enter_context(nc.semaphore("s_" + k))
            self.cnt[k] = 0
        for i in range(self.NDP):
            k = "g%d" % i
            self.sem[k] = es.enter_context(nc.semaphore("s_" + k))
            self.cnt[k] = 0
        self.rrp = 0
        self.known = {e: {} for e in self.ENG}
        self.rr = 0
        self.pe_pending = False
        self.out_dma = []

    def _waits(self, eng, reads, writes, extra=()):
        w = {}

        def need(k, v):
            if v > w.get(k, 0):
                w[k] = v

        for b in reads:
            if b.w is not None:
                need(*b.w)
        for b in writes:
            if b.w is not None:
                need(*b.w)
            for k, v in b.r.items():
                need(k, v)
        for k, v in extra:
            need(k, v)
        out = []
        kn = self.known[eng]
        for k, v in w.items():
            if k == eng and eng == "pe":
                continue
            if kn.get(k, 0) >= v:
                continue
            kn[k] = v
            out.append((k, v))
        return out

    def op(self, eng, fn, reads=(), writes=(), sig=True):
        assert sig or eng == "pe"
        wl = self._waits(eng, reads, writes)
        if sig:
            self.cnt[eng] += 1
            tk = self.cnt[eng]
            if eng == "pe":
                self.pe_pending = False
        else:
            tk = self.cnt[eng] + 1
            self.pe_pending = True
        for b in reads:
            if b.r.get(eng, 0) < tk:
                b.r[eng] = tk
        for b in writes:
            b.w = (eng, tk)
            b.r = {}
        self.q[eng].append((wl, fn, True if sig else None))

    def dma(self, qeng, out_ap, in_ap, reads=(), writes=(), is_out=False):
        if qeng == "pool":
            i = self.rrp
            self.rrp = (i + 1) % self.NDP
            key = "g%d" % i
        else:
            i = self.rr
            self.rr = (i + 1) % self.NDS
            key = "d%d" % i
        prev = self.cnt[key]
        extra = ((key, prev),) if prev > 0 else ()
        wl = self._waits(qeng, reads, writes, extra)
        self.cnt[key] += 16
        tk = self.cnt[key]
        for b in reads:
            b.r[key] = tk
        for b in writes:
            b.w = (key, tk)
            b.r = {}
        self.q[qeng].append((wl, (lambda e, o=out_ap, a=in_ap: e.dma_start(out=o, in_=a)), key))

    def barrier(self):
        assert not self.pe_pending
        snap = {k: v for k, v in self.cnt.items() if v > 0}
        for e in self.ENG:
            wl = []
            kn = self.known[e]
            for k, v in snap.items():
                if k == e and e == "pe":
                    continue
                if kn.get(k, 0) >= v:
                    continue
                kn[k] = v
                wl.append((k, v))
            if wl:
                self.q[e].append((wl, None, None))

    def replay(self, block):
        m = {"sync": block.sync, "act": block.scalar, "dve": block.vector, "pool": block.gpsimd, "pe": block.tensor}
        for e in self.ENG:
            items = self.q[e]

            def body(engobj, items=items, e=e):
                for wl, fn, sig in items:
                    for k, v in wl:
                        engobj.wait_ge(self.sem[k], v)
                    if fn is None:
                        continue
                    ins = fn(engobj)
                    if sig is True:
                        ins.then_inc(self.sem[e], 1)
                    elif sig is not None:
                        ins.then_inc(self.sem[sig], 16)

            m[e](body)


class Arena:
    def __init__(self, ap):
        self.ap = ap
        self.n = ap.shape[1]
        self.off = 0

    def reset(self, off=0):
        self.off = off

    def f32(self, n):
        n4 = (n + 7) // 8 * 8
        assert self.off + n4 <= self.n, ("arena overflow", self.off, n4, self.n)
        v = self.ap[:, self.off:self.off + n]
        self.off += n4
        return v

    def bf16(self, n):
        n2 = (n + 1) // 2
        return self.f32(n2).bitcast(BF16)[:, 0:n]


def build(seqs=(2048, 8192), phases="ABCDE", debug=False):
    T = sum(seqs)
    NT = T // TT
    seq0 = [sum(seqs[:i]) for i in range(len(seqs))]
    nc = bass.Bass("TRN2", target_bir_lowering=False)
    okind = "ExternalOutput" if debug else "Internal"

    def din(name, shape, dt=F32):
        return nc.dram_tensor(name, list(shape), dt, kind="ExternalInput").ap()

    x_d = din("x", [T, D])
    wgu1_d = din("ffn1_w_gu", [D, 2 * DFF])
    wd1_d = din("ffn1_w_down", [DFF, D])
    wgu2_d = din("ffn2_w_gu", [D, 2 * DFF])
    wd2_d = din("ffn2_w_down", [DFF, D])
    win_d = din("w_in", [D, INW])
    wout_d = din("w_out", [D, D])
    gT_d = din("gT", [128, 3 * KC])
    gpost_d = din("gpost", [4, D])
    convw_d = din("convw_t", [512, CK])
    convv_d = din("convv", [128, 12])
    convb_d = din("conv_b", [512])
    ident_d = din("ident", [128, 128])
    bias_d = din("biasT", [128, 3 * NH * 256], BF16)
    y_d = nc.dram_tensor("y", [T, D], F32, kind="ExternalOutput").ap()
    X1 = nc.dram_tensor("X1", [T, D], F32, kind=okind).ap()
    X2 = nc.dram_tensor("X2", [T, D], F32, kind=okind).ap()
    Qs = nc.dram_tensor("Qs", [T, 512], BF16, kind=okind).ap()
    Ks = nc.dram_tensor("Ks", [T, 512], BF16, kind=okind).ap()
    Vs = nc.dram_tensor("Vs", [T, 512], BF16, kind=okind).ap()
    GluT = nc.dram_tensor("GluT", [512, T], BF16, kind=okind).ap()
    AttnT = nc.dram_tensor("AttnT", [512, T], BF16, kind=okind).ap()

    es = contextlib.ExitStack()
    with es:
        NAR = 52480
        arena_t = es.enter_context(nc.sbuf_tensor("arena", [128, NAR], F32))
        cst_t = es.enter_context(nc.sbuf_tensor("cst", [128, 640], F32))
        banks = [es.enter_context(nc.psum_tensor("bank%d" % i, [128, 512], F32)) for i in range(8)]
        pg = Prog(nc, es)
        AR = Arena(arena_t[:, :])
        CS = Arena(cst_t[:, :])

        identf = Buf(CS.f32(128))
        identb = Buf(CS.bf16(128))
        m05 = Buf(CS.f32(8))
        p05 = Buf(CS.f32(8))
        negM = Buf(CS.f32(8 * len(seqs)))
        onesf = Buf(CS.f32(128))
        pg.dma("sync", identf.ap, ident_d[:, :], writes=[identf])
        pg.op("dve", lambda e: e.tensor_copy(out=identb.ap, in_=identf.ap), reads=[identf], writes=[identb])
        pg.op("pool", lambda e: e.memset(m05.ap, -0.5), writes=[m05])
        pg.op("pool", lambda e: e.memset(p05.ap, 0.5), writes=[p05])
        pg.op("pool", lambda e: e.memset(onesf.ap, 1.0), writes=[onesf])
        pg.op("pool", lambda e: e.memset(negM.ap, 0.0), writes=[negM])

        bankf = [Buf(b[:, :]) for b in banks]

        def bbf(i):
            return banks[i][:, :].bitcast(BF16)

        def rstd_from(ss_buf, rstd_buf, v_buf, scale, eps):
            pg.op("pool", lambda e: e.tensor_scalar(out=v_buf.ap, in0=ss_buf.ap, scalar1=scale, scalar2=eps,
                                                    op0=ALU.mult, op1=ALU.add), reads=[ss_buf], writes=[v_buf])
            pg.op("pool", lambda e: e.tensor_tensor(out=rstd_buf.ap, in0=v_buf.ap, in1=m05.ap[:, 0:1], op=ALU.pow),
                  reads=[v_buf, m05], writes=[rstd_buf])

        def make_gbc(gbc, gT, col0):
            for kc in range(KC):
                pg.op("dve", lambda e, kc=kc: e.tensor_scalar(out=gbc.ap[:, kc * 128:(kc + 1) * 128], in0=onesf.ap,
                                                              scalar1=gT.ap[:, col0 + kc:col0 + kc + 1], scalar2=None,
                                                              op0=ALU.mult), reads=[onesf, gT], writes=[gbc])

        class Front:
            def __init__(self, src, gbc, psT_bank, nxin=2, nxh=4, nhT=1, alt_bank=None):
                self.src = src
                self.gbc = gbc
                self.bank = psT_bank
                self.xin = [Buf(AR.f32(D)) for _ in range(nxin)]
                self.xh = [Buf(AR.bf16(D)) for _ in range(nxh)]
                self.st = [[Buf(CS.f32(1)) for _ in range(3)] for _ in range(nxh)]
                self.hTs = [AR.bf16(KC * TT) for _ in range(nhT)]
                self.hTbs = [[Buf() for _ in range(NSUB)] for _ in range(nhT)]
                self.nhT = nhT
                self.alt_bank = alt_bank
                self.tc = 0
                self.hT = self.hTs[0]
                self.hTb = self.hTbs[0]
                self.nxin = nxin
                self.nxh = nxh
                self.ci = 0
                self.ch = 0
                self.pend = {}
                self.lpend = {}

            def load(self, t, s):
                xin = self.xin[self.ci % self.nxin]
                self.ci += 1
                r0 = t * TT + s * 128
                pg.dma("sync", xin.ap, self.src[r0:r0 + 128, :], writes=[xin])
                self.lpend[(t, s)] = xin

            def stats(self, t, s):
                if (t, s) not in self.lpend:
                    self.load(t, s)
                xin = self.lpend.pop((t, s))
                slot = self.ch % self.nxh
                self.ch += 1
                xh = self.xh[slot]
                ss, v, rs = self.st[slot]
                pg.op("act", lambda e: e.activation(out=xh.ap, in_=xin.ap, func=AF.Square, accum_out=ss.ap),
                      reads=[xin], writes=[xh, ss])
                rstd_from(ss, rs, v, 1.0 / D, RMS_EPS)
                pg.op("dve", lambda e: e.tensor_scalar(out=xh.ap, in0=xin.ap, scalar1=rs.ap, scalar2=None, op0=ALU.mult),
                      reads=[xin, rs], writes=[xh])
                self.pend[(t, s)] = xh

            def transp(self, t, s, bank=None):
                xh = self.pend.pop((t, s))
                if bank is None:
                    bank = self.bank
                    if self.alt_bank is not None and self.tc % 2 == 1:
                        bank = self.alt_bank
                self.tc += 1
                pb = bankf[bank]
                pv = bbf(bank)
                for kc in range(KC):
                    pg.op("pe", lambda e, kc=kc: e.transpose(out=pv[:, kc * 128:(kc + 1) * 128],
                                                             in_=xh.ap[:, kc * 128:(kc + 1) * 128], identity=identb.ap),
                          reads=[xh, identb], writes=[pb], sig=(kc == KC - 1))
                hT_ = self.hTs[t % self.nhT]
                hv = hT_.rearrange("p (k t) -> p k t", t=TT)[:, :, s * 128:(s + 1) * 128]
                pg.op("dve", lambda e: e.tensor_tensor(out=hv, in0=pv.rearrange("p (k t) -> p k t", t=128),
                                                       in1=self.gbc.ap.rearrange("p (k t) -> p k t", t=128), op=ALU.mult),
                      reads=[pb, self.gbc], writes=[self.hTbs[t % self.nhT][s]])

            def hk(self, kc, t=0):
                return self.hTs[t % self.nhT][:, kc * TT:(kc + 1) * TT]

            def sel(self, t):
                self.hT = self.hTs[t % self.nhT]
                self.hTb = self.hTbs[t % self.nhT]

        def load_weights_cast(dst_ap3, src2d, nk, ncols, buf, chunk):
            for k in range(nk):
                for c0 in range(0, ncols, chunk):
                    c1 = min(ncols, c0 + chunk)
                    pg.dma("pool", dst_ap3[:, k * ncols + c0:k * ncols + c1], src2d[k * 128:(k + 1) * 128, c0:c1], writes=[buf])

        def epilogue(psA, psB, res_src, r0, gpost, xr, stt, junk, dst, final_g=None, before=None, preloaded=False):
            ssA, ssB, v, rs, ss3, v3, rs3 = stt
            if not preloaded:
                pg.dma("sync", xr.ap, res_src[r0:r0 + 128, :], writes=[xr])
            if before is not None:
                before()
            jv = junk.ap
            pg.op("act", lambda e: e.activation(out=jv[:, 0:512], in_=psA.ap, func=AF.Square, accum_out=ssA.ap),
                  reads=[psA], writes=[junk, ssA])
            pg.op("act", lambda e: e.activation(out=jv[:, 512:1024], in_=psB.ap, func=AF.Square, accum_out=ssB.ap),
                  reads=[psB], writes=[junk, ssB])
            pg.op("pool", lambda e: e.tensor_tensor(out=ssA.ap, in0=ssA.ap, in1=ssB.ap, op=ALU.add), reads=[ssA, ssB], writes=[ssA])
            rstd_from(ssA, rs, v, 1.0 / D, RMS_EPS)
            for nh, ps in enumerate((psA, psB)):
                sl = slice(nh * 512, (nh + 1) * 512)
                pg.op("dve", lambda e, ps=ps, sl=sl: e.scalar_tensor_tensor(out=ps.ap, in0=ps.ap, scalar=rs.ap, in1=gpost.ap[:, sl],
                                                                           op0=ALU.mult, op1=ALU.mult),
                      reads=[ps, rs, gpost], writes=[ps])
                pg.op("dve", lambda e, ps=ps, sl=sl: e.tensor_tensor(out=xr.ap[:, sl], in0=ps.ap, in1=xr.ap[:, sl], op=ALU.add),
                      reads=[ps, xr], writes=[xr])
            def finish():
                if final_g is not None:
                    pg.op("act", lambda e: e.activation(out=jv, in_=xr.ap, func=AF.Square, accum_out=ss3.ap), reads=[xr], writes=[junk, ss3])
                    rstd_from(ss3, rs3, v3, 1.0 / D, RMS_EPS)
                    pg.op("dve", lambda e: e.scalar_tensor_tensor(out=xr.ap, in0=xr.ap, scalar=rs3.ap, in1=final_g.ap,
                                                                  op0=ALU.mult, op1=ALU.mult), reads=[xr, rs3, final_g], writes=[xr])
                pg.dma("sync", dst[r0:r0 + 128, :], xr.ap, reads=[xr])

            if final_g is None:
                finish()
                return None
            return finish

        def load_gpost(buf, row, scale):
            pg.dma("sync", buf.ap, gpost_d[row, :].partition_broadcast(128), writes=[buf])
            if scale != 1.0:
                pg.op("dve", lambda e: e.tensor_scalar(out=buf.ap, in0=buf.ap, scalar1=scale, scalar2=None, op0=ALU.mult),
                      reads=[buf], writes=[buf])

        def ffn_phase(src, wgu_d, wd_d, gcol, gpost_row, dst, final):
            AR.reset()
            CS.reset(cs_mark)
            Wgu = Buf(AR.bf16(KC * 2 * DFF))
            Wd = Buf(AR.bf16(NFC * D))
            PIECES = (1, 1, 2, 3, 4, 11)
            pstart = [sum(PIECES[:i]) for i in range(len(PIECES))]
            fc2p = []
            for i, n in enumerate(PIECES):
                fc2p += [i] * n
            WguB = [[Buf() for _ in PIECES] for _ in range(2)]
            WdB = [Buf() for _ in range(NFC)]
            Wgu3 = Wgu.ap.rearrange("p (k n) -> p k n", n=2 * DFF)
            wsrc3 = wgu_d.rearrange("(k p) n -> p k n", p=128)
            late = []
            for i, n in enumerate(PIECES):
                for gu in range(2):
                    c0 = gu * DFF + pstart[i] * 128
                    f_ = (lambda c0=c0, n=n, gu=gu, i=i: pg.dma("pool", Wgu3[:, :, c0:c0 + n * 128], wsrc3[:, :, c0:c0 + n * 128],
                                                                 writes=[WguB[gu][i]]))
                    if i == 0:
                        f_()
                    else:
                        late.append(f_)
            Wd3 = Wd.ap.rearrange("p (k n) -> p k n", n=D)
            wdsrc3 = wd_d.rearrange("(k p) n -> p k n", p=128)
            for (k0, k1) in ((0, 2), (2, 6), (6, 14), (14, 22)):
                late.append(lambda k0=k0, k1=k1: pg.dma("pool", Wd3[:, k0:k1, :], wdsrc3[:, k0:k1, :], writes=WdB[k0:k1]))
            gT = Buf(CS.f32(3 * KC))
            pg.dma("sync", gT.ap, gT_d[:, :], writes=[gT])
            gbc = Buf(AR.f32(KC * 128))
            make_gbc(gbc, gT, gcol * KC)
            gpost = Buf(AR.f32(D))
            load_gpost(gpost, gpost_row, 0.5)
            gfin = None
            if final:
                gfin = Buf(AR.f32(D))
                load_gpost(gfin, 3, 1.0)
            fr = Front(src, gbc, 0)
            hid = AR.bf16(NFC * TT)
            hidb = [Buf() for _ in range(NFC)]
            sg = [Buf(AR.f32(TT)) for _ in range(2)]
            xr = [Buf(AR.f32(D)) for _ in range(2)]
            junk = Buf(AR.bf16(D))
            stts = [[Buf(CS.f32(1)) for _ in range(7)] for _ in range(2)]
            guring = [bankf[1], bankf[2], bankf[3]]
            gctr = 0
            psD = [bankf[4], bankf[5], bankf[6], bankf[7]]
            nd = 0
            for s in range(NSUB):
                fr.stats(0, s)
            for f_ in late:
                f_()
            for s in range(NSUB):
                fr.transp(0, s)
            ne = 0
            pend_fin = [None]
            for t in range(NT):
                for fc in range(NFC):
                    if fc == 3 and pend_fin[0] is not None:
                        pend_fin[0]()
                        pend_fin[0] = None
                    if t + 1 < NT and fc in (1, 5, 9, 13):
                        fr.load(t + 1, fc // 4)
                    if t + 1 < NT and fc in (4, 8, 12, 16):
                        fr.stats(t + 1, fc // 4 - 1)
                    g, u = guring[gctr % 3], guring[(gctr + 1) % 3]
                    gctr += 2
                    for kc in range(KC):
                        pg.op("pe", lambda e, kc=kc, fc=fc, g=g: e.matmul(
                            g.ap, lhsT=Wgu.ap[:, kc * 2 * DFF + fc * 128: kc * 2 * DFF + (fc + 1) * 128],
                            rhs=fr.hk(kc), start=(kc == 0), stop=(kc == KC - 1)),
                            reads=[WguB[0][fc2p[fc]]] + fr.hTb, writes=[g], sig=(kc == KC - 1))
                    for kc in range(KC):
                        pg.op("pe", lambda e, kc=kc, fc=fc, u=u: e.matmul(
                            u.ap, lhsT=Wgu.ap[:, kc * 2 * DFF + DFF + fc * 128: kc * 2 * DFF + DFF + (fc + 1) * 128],
                            rhs=fr.hk(kc), start=(kc == 0), stop=(kc == KC - 1)),
                            reads=[WguB[1][fc2p[fc]]] + fr.hTb, writes=[u], sig=(kc == KC - 1))
                    sgb = sg[fc % 2]
                    pg.op("act", lambda e, g=g, sgb=sgb: e.activation(out=sgb.ap, in_=g.ap, func=AF.Silu), reads=[g], writes=[sgb])
                    pg.op("dve", lambda e, u=u, sgb=sgb, fc=fc: e.tensor_tensor(out=hid[:, fc * TT:(fc + 1) * TT], in0=u.ap, in1=sgb.ap,
                                                                                op=ALU.mult), reads=[u, sgb], writes=[hidb[fc]])
                if t + 1 < NT:
                    for s in range(NSUB):
                        if s % 2 == 0:
                            fr.transp(t + 1, s)
                        else:
                            fr.transp(t + 1, s, bank=1 + gctr % 3)
                            gctr += 1
                for s in range(NSUB):
                    pss = []
                    for nh in range(2):
                        pd = psD[nd % 4]
                        nd += 1
                        pss.append(pd)
                        for fc in range(NFC):
                            pg.op("pe", lambda e, fc=fc, s=s, nh=nh, pd=pd: e.matmul(
                                pd.ap, lhsT=hid[:, fc * TT + s * 128: fc * TT + (s + 1) * 128],
                                rhs=Wd.ap[:, fc * D + nh * 512: fc * D + (nh + 1) * 512], start=(fc == 0), stop=(fc == NFC - 1)),
                                reads=[WdB[fc]] + hidb, writes=[pd], sig=(fc == NFC - 1))
                    pend_fin[0] = epilogue(pss[0], pss[1], src, t * TT + s * 128, gpost, xr[ne % 2], stts[ne % 2], junk, dst, gfin,
                                           before=pend_fin[0])
                    ne += 1
            if pend_fin[0] is not None:
                pend_fin[0]()
            pg.barrier()

        def phase_b():
            AR.reset()
            CS.reset(cs_mark)
            Win = Buf(AR.bf16(KC * INW))
            WinB = [Buf() for _ in range(5)]
            Win3 = Win.ap.rearrange("p (k n) -> p k n", n=INW)
            winsrc3 = win_d.rearrange("(k p) n -> p k n", p=128)
            for blk in range(5):
                pg.dma("pool", Win3[:, :, blk * 512:(blk + 1) * 512], winsrc3[:, :, blk * 512:(blk + 1) * 512], writes=[WinB[blk]])
            gT = Buf(CS.f32(3 * KC))
            pg.dma("sync", gT.ap, gT_d[:, :], writes=[gT])
            gbc = Buf(AR.f32(KC * 128))
            make_gbc(gbc, gT, 1 * KC)
            fr = Front(X1, gbc, 0, nhT=2, alt_bank=7)
            qst = [Buf(AR.bf16(512)) for _ in range(2)]
            kst = [Buf(AR.bf16(512)) for _ in range(2)]
            vst = [Buf(AR.bf16(512)) for _ in range(2)]
            sq = [Buf(AR.f32(512)) for _ in range(2)]
            sgm = [Buf(AR.f32(512)) for _ in range(2)]
            glst = [Buf(AR.bf16(4 * TT)) for _ in range(2)]
            nrm = [Buf(CS.f32(8)) for _ in range(4)]
            RM = [Buf(CS.f32(8)), Buf(CS.f32(8))]
            mx = Buf(CS.f32(2))
            m2 = Buf(CS.f32(1))
            Mv = Buf(CS.f32(1))
            dg = Buf(CS.f32(8))
            nm8 = Buf(CS.f32(8))
            ring = [bankf[i] for i in range(1, 7)]
            rc = [0]

            def nxt():
                b = ring[rc[0] % len(ring)]
                rc[0] += 1
                return b

            cq = 0
            for s in range(NSUB):
                fr.stats(0, s)
            for s in range(NSUB):
                fr.transp(0, s)
            for si, S in enumerate(seqs):
                for rm in RM:
                    pg.op("pool", lambda e, rm=rm: e.memset(rm.ap, 0.0), writes=[rm])
                for tl in range(S // TT):
                    t = seq0[si] // TT + tl
                    fr.sel(t)
                    for s in range(NSUB):
                        if t + 1 < NT:
                            fr.load(t + 1, s)
                        r0 = t * TT + s * 128
                        outs = []
                        for blk in range(3):
                            ps = nxt()
                            outs.append(ps)
                            for kc in range(KC):
                                pg.op("pe", lambda e, kc=kc, s=s, blk=blk, ps=ps, hT_=fr.hT: e.matmul(
                                    ps.ap, lhsT=hT_[:, kc * TT + s * 128: kc * TT + (s + 1) * 128],
                                    rhs=Win.ap[:, kc * INW + blk * 512: kc * INW + (blk + 1) * 512],
                                    start=(kc == 0), stop=(kc == KC - 1)), reads=[WinB[blk], fr.hTb[s]], writes=[ps], sig=(kc == KC - 1))
                        for wi, (ps, stg, scale, dstT) in enumerate(((outs[0], qst[cq % 2], 0.125, Qs), (outs[1], kst[cq % 2], 1.0, Ks))):
                            sqb = sq[wi]
                            nb = nrm[(cq % 2) * 2 + wi]
                            pg.op("act", lambda e, ps=ps, stg=stg, scale=scale: e.activation(out=stg.ap, in_=ps.ap, func=AF.Copy, scale=scale),
                                  reads=[ps], writes=[stg])
                            pg.op("act", lambda e, stg=stg, sqb=sqb: e.activation(out=sqb.ap, in_=stg.ap, func=AF.Square), reads=[stg], writes=[sqb])
                            pg.op("dve", lambda e, sqb=sqb, nb=nb: e.tensor_reduce(out=nb.ap, in_=sqb.ap.rearrange("p (h d) -> p h d", d=64),
                                                                                   axis=AX.X, op=ALU.add), reads=[sqb], writes=[nb])
                            pg.op("dve", lambda e, nb=nb, wi=wi: e.tensor_tensor(out=RM[wi].ap, in0=RM[wi].ap, in1=nb.ap, op=ALU.max),
                                  reads=[nb, RM[wi]], writes=[RM[wi]])
                            pg.dma("sync", dstT[r0:r0 + 128, :], stg.ap, reads=[stg])
                        vb = vst[cq % 2]
                        pg.op("dve", lambda e, vb=vb, ps=outs[2]: e.tensor_copy(out=vb.ap, in_=ps.ap), reads=[outs[2]], writes=[vb])
                        pg.dma("sync", Vs[r0:r0 + 128, :], vb.ap, reads=[vb])
                        cq += 1
                        if t + 1 < NT and s >= 1:
                            fr.stats(t + 1, s - 1)
                    gl = glst[tl % 2]
                    for c in range(4):
                        if c == 1 and t + 1 < NT:
                            fr.stats(t + 1, 3)
                        if c == 3 and t + 1 < NT:
                            for s in range(NSUB):
                                fr.transp(t + 1, s)
                        pa, pb = nxt(), nxt()
                        for (ps, col) in ((pa, 1536 + c * 128), (pb, 2048 + c * 128)):
                            for kc in range(KC):
                                pg.op("pe", lambda e, kc=kc, ps=ps, col=col, t=t: e.matmul(
                                    ps.ap, lhsT=Win.ap[:, kc * INW + col: kc * INW + col + 128], rhs=fr.hk(kc, t),
                                    start=(kc == 0), stop=(kc == KC - 1)), reads=[WinB[col // 512]] + fr.hTb, writes=[ps], sig=(kc == KC - 1))
                        sb_ = sgm[c % 2]
                        pg.op("act", lambda e, pb=pb, sb_=sb_: e.activation(out=sb_.ap, in_=pb.ap, func=AF.Sigmoid), reads=[pb], writes=[sb_])
                        pg.op("dve", lambda e, pa=pa, sb_=sb_, gl=gl, c=c: e.tensor_tensor(out=gl.ap[:, c * TT:(c + 1) * TT], in0=pa.ap, in1=sb_.ap,
                                                                                          op=ALU.mult), reads=[pa, sb_], writes=[gl])
                    pg.dma("sync", GluT.rearrange("(c p) t -> p c t", p=128)[:, :, t * TT:(t + 1) * TT],
                           gl.ap.rearrange("p (c t) -> p c t", t=TT), reads=[gl])
                px = nxt()
                for wi in range(2):
                    pg.op("pe", lambda e, wi=wi, px=px: e.transpose(out=px.ap[0:8, wi * 128:(wi + 1) * 128], in_=RM[wi].ap, identity=identf.ap),
                          reads=[RM[wi], identf], writes=[px], sig=True)
                pg.op("dve", lambda e, px=px: e.tensor_reduce(out=mx.ap[0:8, :], in_=px.ap[0:8, 0:256].rearrange("p (a b) -> p a b", b=128),
                                                              axis=AX.X, op=ALU.max), reads=[px], writes=[mx])
                pg.op("dve", lambda e: e.tensor_tensor(out=m2.ap[0:8, :], in0=mx.ap[0:8, 0:1], in1=mx.ap[0:8, 1:2], op=ALU.mult),
                      reads=[mx], writes=[m2])
                pg.op("pool", lambda e: e.tensor_tensor(out=Mv.ap[0:8, :], in0=m2.ap[0:8, :], in1=p05.ap[0:8, 0:1], op=ALU.pow),
                      reads=[m2, p05], writes=[Mv])
                pg.op("dve", lambda e: e.tensor_scalar(out=dg.ap[0:8, :], in0=identf.ap[0:8, 0:8], scalar1=Mv.ap[0:8, :], scalar2=-1.001,
                                                       op0=ALU.mult, op1=ALU.mult), reads=[identf, Mv], writes=[dg])
                px2 = nxt()
                pg.op("pe", lambda e, px2=px2: e.matmul(px2.ap[:, 0:8], lhsT=onesf.ap[0:8, :], rhs=dg.ap[0:8, :], start=True, stop=True),
                      reads=[onesf, dg], writes=[px2], sig=True)
                pg.op("dve", lambda e, px2=px2: e.tensor_copy(out=nm8.ap, in_=px2.ap[:, 0:8]), reads=[px2], writes=[nm8])
                pg.op("dve", lambda e, si=si: e.tensor_tensor(out=negM.ap[:, si * 4:(si + 1) * 4], in0=nm8.ap[:, 0:8:2], in1=nm8.ap[:, 1:8:2],
                                                              op=ALU.min), reads=[nm8], writes=[negM])
            pg.barrier()

        def phase_c():
            AR.reset()
            CS.reset(cs_mark)
            Accs = [AR.f32(NH * SB) for _ in range(2)]
            Acc3s = [a_.rearrange("p (h t) -> p h t", t=SB) for a_ in Accs]
            accbs = [[Buf() for _ in range(4)] for _ in range(2)]
            biasb = Buf(AR.bf16(3 * NH * 256))
            pg.dma("sync", biasb.ap, bias_d[:, :], writes=[biasb])
            aouts = [Buf(AR.bf16(SB)) for _ in range(2)]
            Rbs = [Buf(AR.f32(SB)) for _ in range(2)]
            qland = [Buf(AR.bf16(512)) for _ in range(2)]
            kland = [Buf(AR.bf16(512)) for _ in range(4)]
            vland = [Buf(AR.bf16(512)) for _ in range(4)]
            qT = [Buf(AR.bf16(1024)) for _ in range(2)]
            pT = [Buf(AR.bf16(512)) for _ in range(3)]

            class Slot:
                def __init__(self):
                    self.kT = Buf(AR.bf16(512))
                    self.va = Buf(AR.bf16(1024))

            slots = {"N": [Slot() for _ in range(4)], "F": [Slot() for _ in range(2)], "L": [Slot() for _ in range(2)]}
            sctr = {"N": 0, "F": 0, "L": 0}
            for b in qT + kland + qland + vland:
                pg.op("pool", lambda e, b=b: e.memset(b.ap, 0.0), writes=[b])
            for sl in slots["N"]:
                pg.op("pool", lambda e, sl=sl: e.memset(sl.va.ap, 1.0), writes=[sl.va])
            for typ, (vlo, vhi) in (("F", (64, 128)), ("L", (0, 64))):
                for sl in slots[typ]:
                    pg.op("pool", lambda e, sl=sl: e.memset(sl.va.ap, 0.0), writes=[sl.va])
                    pg.op("pool", lambda e, sl=sl: e.memset(sl.kT.ap, 0.0), writes=[sl.kT])
                    v3 = sl.va.ap.rearrange("p (h d) -> p h d", d=128)
                    pg.op("dve", lambda e, v3=v3, vlo=vlo, vhi=vhi: e.memset(v3[vlo:vhi, 0:8:2, 64:128], 1.0), writes=[sl.va])
                    pg.op("dve", lambda e, v3=v3, vlo=vlo, vhi=vhi: e.memset(v3[vlo:vhi, 1:8:2, 0:64], 1.0), writes=[sl.va])
            psT = [0, 1]
            psS = [bankf[2], bankf[3], bankf[4]]
            psO = [bankf[5], bankf[6], bankf[7]]
            tctr = [0]

            tiles = []
            keys = []
            sbg = -1
            for si, S in enumerate(seqs):
                for sb in range(S // SB):
                    b0 = sb * SB
                    sbg += 1
                    for di, d in enumerate(DILS):
                        L = S // d
                        nt = SB // d // 128
                        for r in range(d):
                            base = len(keys)
                            for j in range(nt + 1):
                                Uk = b0 // d + 128 * j - 64
                                lo = max(0, -Uk)
                                hi = min(128, L - Uk)
                                typ = "F" if lo > 0 else ("L" if hi < 128 else "N")
                                keys.append(dict(tk=seq0[si] + r + d * (Uk + lo), lo=lo, hi=hi, typ=typ, d=d))
                            for i in range(nt):
                                tiles.append(dict(si=si, sbk=(si, sb), sbg=sbg, di=di, d=d, r=r, i=i, k0=base + i, k1=base + i + 1,
                                                  tq=seq0[si] + b0 + r + d * 128 * i, tb=seq0[si] + b0))
            NTL = len(tiles)
            lctr = [0]

            def load_key(m):
                k = keys[m]
                if "land" in k:
                    return
                li = lctr[0] % 4
                lctr[0] += 1
                k["land"] = li
                kl, vl = kland[li], vland[li]
                lo, hi, d = k["lo"], k["hi"], k["d"]
                n = hi - lo
                if n < 128:
                    zlo = 0 if lo > 0 else hi
                    pg.op("pool", lambda e: e.memset(kl.ap[zlo:zlo + 64, :], 0.0), writes=[kl])
                pg.dma("sync", kl.ap[lo:hi, :], Ks[sl_(k["tk"], n, d), :], writes=[kl])
                pg.dma("sync", vl.ap[lo:hi, :], Vs[sl_(k["tk"], n, d), :], writes=[vl])

            def tr_key(m):
                k = keys[m]
                if "slot" in k:
                    return
                typ = k["typ"]
                sl = slots[typ][sctr[typ] % len(slots[typ])]
                sctr[typ] += 1
                k["slot"] = sl
                kl, vl = kland[k["land"]], vland[k["land"]]
                bi = psT[tctr[0] % 2]
                tctr[0] += 1
                pb, pv = bankf[bi], bbf(bi)
                for c in range(4):
                    pg.op("pe", lambda e, c=c: e.transpose(out=pv[:, c * 128:(c + 1) * 128], in_=kl.ap[:, c * 128:(c + 1) * 128],
                                                           identity=identb.ap), reads=[kl, identb], writes=[pb], sig=(c == 3))
                if tctr[0] % 4 < 2:
                    pg.op("act", lambda e: e.activation(out=sl.kT.ap, in_=pv[:, 0:512], func=AF.Copy), reads=[pb], writes=[sl.kT])
                else:
                    pg.op("dve", lambda e: e.tensor_copy(out=sl.kT.ap, in_=pv[:, 0:512]), reads=[pb], writes=[sl.kT])
                lo, hi = k["lo"], k["hi"]
                v3 = sl.va.ap.rearrange("p (h d) -> p h d", d=128)
                l3 = vl.ap.rearrange("p (h d) -> p h d", d=64)
                if hi - lo == 128:
                    pg.op("pool", lambda e: e.tensor_copy(out=v3[:, 0:8:2, 0:64], in_=l3[:, 0:8:2, :]), reads=[vl], writes=[sl.va])
                    pg.op("pool", lambda e: e.tensor_copy(out=v3[:, 1:8:2, 64:128], in_=l3[:, 1:8:2, :]), reads=[vl], writes=[sl.va])
                else:
                    pg.op("act", lambda e: e.activation(out=v3[lo:hi, 0:8:2, 0:64], in_=l3[lo:hi, 0:8:2, :], func=AF.Copy), reads=[vl], writes=[sl.va])
                    pg.op("act", lambda e: e.activation(out=v3[lo:hi, 1:8:2, 64:128], in_=l3[lo:hi, 1:8:2, :], func=AF.Copy), reads=[vl], writes=[sl.va])

            def load_tile(n):
                tl = tiles[n]
                qb = qland[n % 2]
                pg.dma("sync", qb.ap, Qs[sl_(tl["tq"], 128, tl["d"]), :], writes=[qb])
                load_key(tl["k0"])
                load_key(tl["k1"])

            def tr_tile(n):
                tl = tiles[n]
                qb = qland[n % 2]
                qtb = qT[n % 2]
                bi = psT[tctr[0] % 2]
                tctr[0] += 1
                pb, pv = bankf[bi], bbf(bi)
                for c in range(4):
                    pg.op("pe", lambda e, c=c: e.transpose(out=pv[:, c * 128:(c + 1) * 128], in_=qb.ap[:, c * 128:(c + 1) * 128],
                                                           identity=identb.ap), reads=[qb, identb], writes=[pb], sig=(c == 3))
                for hh in range(2):
                    pg.op("dve", lambda e, hh=hh: e.tensor_copy(
                        out=qtb.ap[64 * hh:64 * hh + 64, :].rearrange("p (c v q) -> p c v q", v=2, q=128)[:, :, hh, :],
                        in_=pv[64 * hh:64 * hh + 64, 0:512].rearrange("p (c q) -> p c q", q=128)), reads=[pb], writes=[qtb])
                tr_key(tl["k0"])
                tr_key(tl["k1"])

            P = [(n, hp) for n in range(NTL) for hp in range(4)]

            def emit_S(idx):
                n, hp = P[idx]
                tl = tiles[n]
                qtb = qT[n % 2]
                ks = (keys[tl["k0"]]["slot"], keys[tl["k1"]]["slot"])
                ps, pt = psS[idx % 3], pT[idx % 3]
                di, si = tl["di"], tl["si"]
                for hh in range(2):
                    h = 2 * hp + hh
                    bo = (di * NH + h) * 256
                    pg.op("pe", lambda e, bo=bo, hh=hh: e.matmul(ps.ap[:, hh * 256:(hh + 1) * 256], lhsT=identb.ap,
                                                                 rhs=biasb.ap[:, bo:bo + 256], start=True, stop=False),
                          reads=[identb, biasb], writes=[ps], sig=False)
                    for kt in range(2):
                        pg.op("pe", lambda e, kt=kt, hh=hh: e.matmul(
                            ps.ap[:, hh * 256 + kt * 128: hh * 256 + (kt + 1) * 128],
                            lhsT=ks[kt].kT.ap[:, hp * 128:(hp + 1) * 128],
                            rhs=qtb.ap[:, (hp * 2 + hh) * 128:(hp * 2 + hh + 1) * 128], start=False, stop=(kt == 1)),
                            reads=[ks[kt].kT, qtb], writes=[ps], sig=(kt == 1 and hh == 1))
                pg.op("act", lambda e: e.activation(out=pt.ap, in_=ps.ap, func=AF.Exp, bias=negM.ap[:, si * 4 + hp: si * 4 + hp + 1]),
                      reads=[ps, negM], writes=[pt])

            def emit_PV(idx):
                n, hp = P[idx]
                tl = tiles[n]
                ks = (keys[tl["k0"]]["slot"], keys[tl["k1"]]["slot"])
                pt, po = pT[idx % 3], psO[idx % 3]
                d, r, i, di = tl["d"], tl["r"], tl["i"], tl["di"]
                for hh in range(2):
                    h = 2 * hp + hh
                    for kt in range(2):
                        pg.op("pe", lambda e, kt=kt, hh=hh, h=h: e.matmul(
                            po.ap[:, hh * 128:(hh + 1) * 128], lhsT=ks[kt].va.ap[:, h * 128:(h + 1) * 128],
                            rhs=pt.ap[:, hh * 256 + kt * 128: hh * 256 + (kt + 1) * 128], start=(kt == 0), stop=(kt == 1)),
                            reads=[ks[kt].va, pt], writes=[po], sig=(kt == 1 and hh == 1))
                Acc3 = Acc3s[tl["sbg"] % 2]
                accb = accbs[tl["sbg"] % 2]
                av = Acc3[:, 2 * hp:2 * hp + 2, sl_(r + d * 128 * i, 128, d)]
                pv_ = po.ap[:, 0:256].rearrange("p (h t) -> p h t", t=128)
                if di == 0:
                    pg.op("dve", lambda e: e.tensor_copy(out=av, in_=pv_), reads=[po], writes=[accb[hp]])
                else:
                    pg.op("dve", lambda e: e.tensor_tensor(out=av, in0=pv_, in1=av, op=ALU.add), reads=[po, accb[hp]], writes=[accb[hp]])

            def norm_steps(tl, h):
                Acc = Accs[tl["sbg"] % 2]
                accb = accbs[tl["sbg"] % 2]
                c = h // 2
                nlo, dlo = (0, 64) if h % 2 == 0 else (64, 0)
                Av = Acc[:, h * SB:(h + 1) * SB]
                Rb = Rbs[h % 2]
                ao = aouts[c % 2]
                HS = SB // 2
                steps = []
                for hf in range(2):
                    cs = slice(hf * HS, (hf + 1) * HS)
                    steps.append(lambda cs=cs: pg.op("act", lambda e: e.activation(out=Rb.ap[nlo:nlo + 64, cs], in_=Av[dlo:dlo + 64, cs], func=AF.Ln),
                                                     reads=[accb[c]], writes=[Rb]))
                    steps.append(lambda cs=cs: pg.op("act", lambda e: e.activation(out=Rb.ap[nlo:nlo + 64, cs], in_=Rb.ap[nlo:nlo + 64, cs], func=AF.Exp,
                                                                                  scale=-1.0), reads=[Rb], writes=[Rb]))
                    steps.append(lambda cs=cs: pg.op("dve", lambda e: e.tensor_tensor(out=ao.ap[nlo:nlo + 64, cs], in0=Av[nlo:nlo + 64, cs],
                                                                                     in1=Rb.ap[nlo:nlo + 64, cs], op=ALU.mult),
                                                     reads=[accb[c], Rb], writes=[ao]))
                if h % 2 == 1:
                    tb = tl["tb"]
                    steps.append(lambda: pg.dma("sync", AttnT[c * 128:(c + 1) * 128, tb:tb + SB], ao.ap, reads=[ao]))
                return steps

            pend_norm = []
            load_tile(0)
            if NTL > 1:
                load_tile(1)
            tr_tile(0)
            emit_S(0)
            emit_S(1)
            for idx, (n, hp) in enumerate(P):
                if hp == 0:
                    if n + 1 < NTL:
                        tr_tile(n + 1)
                    if n + 2 < NTL:
                        load_tile(n + 2)
                if idx + 2 < len(P):
                    emit_S(idx + 2)
                emit_PV(idx)
                if pend_norm:
                    pend_norm.pop(0)()
                if hp == 3 and (n + 1 == NTL or tiles[n + 1]["sbk"] != tiles[n]["sbk"]):
                    while pend_norm:
                        pend_norm.pop(0)()
                    for h in range(NH):
                        pend_norm.extend(norm_steps(tiles[n], h))
            while pend_norm:
                pend_norm.pop(0)()
            pg.barrier()

        def phase_d():
            AR.reset()
            CS.reset(cs_mark)
            Wout = Buf(AR.bf16(KC * D))
            cwT = Buf(AR.f32(4 * CK))
            pg.dma("sync", cwT.ap.rearrange("p (c j) -> p c j", j=CK), convw_d.rearrange("(c p) j -> p c j", p=128), writes=[cwT])
            diag = Buf(AR.bf16(CK * 4 * 128))
            diagb = [Buf() for _ in range(CK * 4)]
            for j in range(CK):
                for c in range(4):
                    o = (j * 4 + c) * 128
                    if (j * 4 + c) % 2 == 0:
                        pg.op("dve", lambda e, o=o, c=c, j=j: e.tensor_scalar(out=diag.ap[:, o:o + 128], in0=identf.ap,
                                                                              scalar1=cwT.ap[:, c * CK + j: c * CK + j + 1], scalar2=None,
                                                                              op0=ALU.mult), reads=[identf, cwT], writes=[diagb[j * 4 + c]])
                    else:
                        pg.op("act", lambda e, o=o, c=c, j=j: e.activation(out=diag.ap[:, o:o + 128], in_=identf.ap, func=AF.Copy,
                                                                           scale=cwT.ap[:, c * CK + j: c * CK + j + 1]),
                              reads=[identf, cwT], writes=[diagb[j * 4 + c]])
            cv = Buf(CS.f32(12))
            pg.dma("sync", cv.ap, convv_d[:, :], writes=[cv])
            ones2 = Buf(CS.f32(2))
            pg.op("pool", lambda e: e.memset(ones2.ap, 1.0), writes=[ones2])
            gpost = Buf(AR.f32(D))
            load_gpost(gpost, 1, 1.0)
            WN = TT + CK - 1
            gw = [Buf(AR.bf16(4 * WN + 2)) for _ in range(3)]
            at = [Buf(AR.bf16(4 * TT)) for _ in range(4)]
            loaded_d = set()
            cT = [AR.f32(4 * TT) for _ in range(2)]
            cTb = [[Buf() for _ in range(4)] for _ in range(2)]
            csq = [AR.f32(4 * TT) for _ in range(2)]
            csqb = [[Buf() for _ in range(4)] for _ in range(2)]
            convT = [AR.bf16(4 * TT) for _ in range(2)]
            convTb = [[Buf() for _ in range(4)] for _ in range(2)]
            dR = [Buf(AR.f32(128)) for _ in range(2)]
            dS = [Buf(AR.f32(128)) for _ in range(2)]
            sm = [[Buf(CS.f32(4)) for _ in range(6)] for _ in range(2)]
            xr = [Buf(AR.f32(D)) for _ in range(3)]
            junk = Buf(AR.bf16(D))
            stts = [[Buf(CS.f32(1)) for _ in range(7)] for _ in range(2)]
            xr_issued = set()

            def xr_load(e):
                if e in xr_issued or e >= NTD * NSUB:
                    return
                xr_issued.add(e)
                si_, tl_ = tl_list[e // NSUB]
                r0_ = seq0[si_] + tl_ * TT + (e % NSUB) * 128
                pg.dma("sync", xr[e % 3].ap, X1[r0_:r0_ + 128, :], writes=[xr[e % 3]])

            psC = [bankf[0], bankf[1]]
            bcR, bcS = bankf[2], bankf[3]
            pst = bcS
            psO = [bankf[4], bankf[5], bankf[6], bankf[7]]
            tl_list = []
            for si, S in enumerate(seqs):
                for tl in range(S // TT):
                    tl_list.append((si, tl))
            NTD = len(tl_list)
            GT = GluT.rearrange("(c p) t -> p c t", p=128)
            AT3 = AttnT.rearrange("(c p) t -> p c t", p=128)
            cc = [0]
            oc = [0]
            ec = [0]

            def dload(t):
                if t in loaded_d or t >= NTD:
                    return
                loaded_d.add(t)
                si, tl = tl_list[t]
                S = seqs[si]
                t0 = seq0[si] + tl * TT
                g = gw[t % 3]
                gv = g.ap[:, 0:4 * WN].rearrange("p (c w) -> p c w", w=WN)
                lo, hi = 0, WN
                if tl == 0:
                    lo = 15
                    pg.op("pool", lambda e: e.memset(gv[:, :, 0:15], 0.0), writes=[g])
                if tl == S // TT - 1:
                    hi = WN - 15
                    pg.op("pool", lambda e: e.memset(gv[:, :, WN - 15:WN], 0.0), writes=[g])
                pg.dma("sync", gv[:, :, lo:hi], GT[:, :, t0 - 15 + lo:t0 - 15 + hi], writes=[g])
                a = at[t % 4]
                pg.dma("sync", a.ap.rearrange("p (c t) -> p c t", t=TT), AT3[:, :, t0:t0 + TT], writes=[a])

            def conv(t):
                dload(t)
                dload(t + 1)
                g = gw[t % 3]
                k = t % 2
                for c in range(4):
                    pc = psC[cc[0] % 2]
                    cc[0] += 1
                    for j in range(CK):
                        o = (j * 4 + c) * 128
                        pg.op("pe", lambda e, c=c, j=j, o=o, pc=pc: e.matmul(
                            pc.ap, lhsT=diag.ap[:, o:o + 128], rhs=g.ap[:, c * WN + j: c * WN + j + TT],
                            start=(j == 0), stop=(j == CK - 1)), reads=[g, diagb[j * 4 + c]], writes=[pc], sig=(j == CK - 1))
                    pg.op("act", lambda e, c=c, pc=pc: e.activation(out=cT[k][:, c * TT:(c + 1) * TT], in_=pc.ap, func=AF.Identity,
                                                                    bias=cv.ap[:, 8 + c:9 + c]), reads=[pc, cv], writes=[cTb[k][c]])
                    pg.op("act", lambda e, c=c, pc=pc: e.activation(out=csq[k][:, c * TT:(c + 1) * TT], in_=pc.ap, func=AF.Square,
                                                                    bias=cv.ap[:, 8 + c:9 + c]), reads=[pc, cv], writes=[csqb[k][c]])

            def stats(t):
                k = t % 2
                for s in range(NSUB):
                    for gi, (src, srcb) in enumerate(((cT[k], cTb[k]), (csq[k], csqb[k]))):
                        for c in range(4):
                            pg.op("pe", lambda e, c=c, s=s, gi=gi, src=src: e.matmul(
                                pst.ap[:, gi * 8 + s * 2: gi * 8 + s * 2 + 2], lhsT=src[:, c * TT + s * 128: c * TT + (s + 1) * 128],
                                rhs=ones2.ap, start=(c == 0), stop=(c == 3)), reads=[srcb[c], ones2], writes=[pst],
                                sig=(c == 3 and gi == 1 and s == NSUB - 1))
                mean, msq, var, vv, rstd, shift = sm[k]
                pg.op("dve", lambda e: e.tensor_scalar(out=mean.ap, in0=pst.ap[:, 0:8:2], scalar1=1.0 / 512, scalar2=None, op0=ALU.mult),
                      reads=[pst], writes=[mean])
                pg.op("dve", lambda e: e.tensor_tensor(out=msq.ap, in0=mean.ap, in1=mean.ap, op=ALU.mult), reads=[mean], writes=[msq])
                pg.op("dve", lambda e: e.scalar_tensor_tensor(out=var.ap, in0=pst.ap[:, 8:16:2], scalar=1.0 / 512, in1=msq.ap,
                                                              op0=ALU.mult, op1=ALU.subtract), reads=[pst, msq], writes=[var])
                pg.op("pool", lambda e: e.tensor_scalar(out=vv.ap, in0=var.ap, scalar1=1.0, scalar2=LN_EPS, op0=ALU.mult, op1=ALU.add),
                      reads=[var], writes=[vv])
                pg.op("pool", lambda e: e.tensor_tensor(out=rstd.ap, in0=vv.ap, in1=m05.ap[:, 0:4], op=ALU.pow), reads=[vv, m05], writes=[rstd])
                pg.op("dve", lambda e: e.scalar_tensor_tensor(out=shift.ap, in0=mean.ap, scalar=-1.0, in1=rstd.ap, op0=ALU.mult, op1=ALU.mult),
                      reads=[mean, rstd], writes=[shift])

            def bc_norm(t):
                k = t % 2
                mean, msq, var, vv, rstd, shift = sm[k]
                for s in range(NSUB):
                    r_, s_ = dR[s % 2], dS[s % 2]
                    pg.op("dve", lambda e, s=s, r_=r_: e.tensor_scalar(out=r_.ap, in0=identf.ap, scalar1=rstd.ap[:, s:s + 1], scalar2=None, op0=ALU.mult),
                          reads=[identf, rstd], writes=[r_])
                    pg.op("dve", lambda e, s=s, s_=s_: e.tensor_scalar(out=s_.ap, in0=identf.ap, scalar1=shift.ap[:, s:s + 1], scalar2=None, op0=ALU.mult),
                          reads=[identf, shift], writes=[s_])
                    pg.op("pe", lambda e, s=s, r_=r_: e.matmul(bcR.ap[:, s * 128:(s + 1) * 128], lhsT=onesf.ap, rhs=r_.ap, start=True, stop=True),
                          reads=[onesf, r_], writes=[bcR], sig=True)
                    pg.op("pe", lambda e, s=s, s_=s_: e.matmul(bcS.ap[:, s * 128:(s + 1) * 128], lhsT=onesf.ap, rhs=s_.ap, start=True, stop=True),
                          reads=[onesf, s_], writes=[bcS], sig=True)
                for c in range(4):
                    cv_ = cT[k][:, c * TT:(c + 1) * TT]
                    pg.op("dve", lambda e, cv_=cv_: e.tensor_tensor(out=cv_, in0=cv_, in1=bcR.ap, op=ALU.mult), reads=[bcR, cTb[k][c]], writes=[cTb[k][c]])
                    pg.op("dve", lambda e, cv_=cv_: e.tensor_tensor(out=cv_, in0=cv_, in1=bcS.ap, op=ALU.add), reads=[bcS, cTb[k][c]], writes=[cTb[k][c]])
                    pg.op("act", lambda e, cv_=cv_, c=c: e.activation(out=convT[k][:, c * TT:(c + 1) * TT], in_=cv_, func=AF.Silu,
                                                                      scale=cv.ap[:, c:c + 1], bias=cv.ap[:, 4 + c:5 + c]),
                          reads=[cTb[k][c], cv], writes=[convTb[k][c]])

            def outproj(t):
                si, tl = tl_list[t]
                k = t % 2
                a = at[t % 4]
                for s in range(NSUB):
                    pss = []
                    for nh in range(2):
                        po = psO[oc[0] % 4]
                        oc[0] += 1
                        pss.append(po)
                        for kc in range(KC):
                            src = a.ap if kc < 4 else convT[k]
                            kk = kc % 4
                            pg.op("pe", lambda e, kc=kc, kk=kk, src=src, nh=nh, po=po, s=s: e.matmul(
                                po.ap, lhsT=src[:, kk * TT + s * 128: kk * TT + (s + 1) * 128],
                                rhs=Wout.ap[:, kc * D + nh * 512: kc * D + (nh + 1) * 512], start=(kc == 0), stop=(kc == KC - 1)),
                                reads=[a, Wout] + convTb[k], writes=[po], sig=(kc == KC - 1))
                    r0 = seq0[si] + tl * TT + s * 128
                    xr_load(ec[0])
                    xr_load(ec[0] + 1)
                    epilogue(pss[0], pss[1], X1, r0, gpost, xr[ec[0] % 3], stts[ec[0] % 2], junk, X2, None, preloaded=True)
                    ec[0] += 1

            dload(0)
            dload(1)
            pg.dma("pool", Wout.ap.rearrange("p (k n) -> p k n", n=D), wout_d.rearrange("(k p) n -> p k n", p=128), writes=[Wout])
            conv(0)
            for t in range(NTD + 1):
                if t < NTD:
                    stats(t)
                if t + 1 < NTD:
                    conv(t + 1)
                if t < NTD:
                    bc_norm(t)
                if t >= 1:
                    outproj(t - 1)
            pg.barrier()

        cs_mark = CS.off
        if "A" in phases:
            ffn_phase(x_d, wgu1_d, wd1_d, 0, 0, X1, False)
        if "B" in phases:
            phase_b()
        if "C" in phases:
            phase_c()
        if "D" in phases:
            phase_d()
        if "E" in phases:
            ffn_phase(X2, wgu2_d, wd2_d, 2, 2, y_d, True)
        pg.barrier()
        with nc.Block() as block:
            pg.replay(block)
    return nc


def host_consts():
    ident = np.eye(128, dtype=np.float32)
    p = np.arange(128)[:, None, None]
    kt = np.arange(2)[None, :, None]
    q = np.arange(128)[None, None, :]
    rel = (p - 64 + 128 * kt) - q
    arel = np.abs(rel).astype(np.float32)
    slopes = 2.0 ** (-8.0 * np.arange(1, NH + 1, dtype=np.float32) / NH)
    tab = np.zeros((128, 3, NH, 2, 128), np.float32)
    for di, d in enumerate(DILS):
        for h in range(NH):
            b = -(slopes[h] * d) * arel
            tab[:, di, h] = np.where(arel <= 64, b, MASKV)
    return ident, tab.reshape(128, 3 * NH * 256).astype(ml_dtypes.bfloat16)


def make_common(ffn1_pre_g, ffn1_w_gu, ffn1_w_down, ffn1_post_g, mix_pre_g, w_in, conv_w, conv_b, conv_ln_g, conv_ln_b,
                w_out, mix_post_g, ffn2_pre_g, ffn2_w_gu, ffn2_w_down, ffn2_post_g, final_g):
    f = lambda a: np.ascontiguousarray(np.asarray(a, dtype=np.float32))
    ident, biasT = host_consts()
    gT = np.concatenate([f(g).reshape(KC, 128).T for g in (ffn1_pre_g, mix_pre_g, ffn2_pre_g)], axis=1)
    gpost = np.stack([f(g).reshape(D) for g in (ffn1_post_g, mix_post_g, ffn2_post_g, final_g)], axis=0)
    convv = np.concatenate([f(conv_ln_g).reshape(4, 128).T, f(conv_ln_b).reshape(4, 128).T, f(conv_b).reshape(4, 128).T], axis=1)
    return {
        "ffn1_w_gu": f(ffn1_w_gu).reshape(D, 2 * DFF), "ffn1_w_down": f(ffn1_w_down).reshape(DFF, D),
        "ffn2_w_gu": f(ffn2_w_gu).reshape(D, 2 * DFF), "ffn2_w_down": f(ffn2_w_down).reshape(DFF, D),
        "w_in": f(w_in).reshape(D, INW), "w_out": f(w_out).reshape(D, D),
        "gT": f(gT), "gpost": f(gpost), "convw_t": f(f(conv_w).reshape(CK, 512).T), "convv": f(convv),
        "conv_b": f(conv_b).reshape(512), "ident": ident, "biasT": biasT,
    }


_NC_CACHE = {}


def kernel(x_prompt, x_sample, **w):
    xp = np.asarray(x_prompt, dtype=np.float32)
    xs = np.asarray(x_sample, dtype=np.float32)
    n = 8
    common = make_common(**w)
    if "nc" not in _NC_CACHE:
        _NC_CACHE["nc"] = build()
    nc = _NC_CACHE["nc"]
    in_maps = []
    for i in range(n):
        m = dict(common)
        m["x"] = np.concatenate([xp[i], xs[i]], axis=0)
        in_maps.append(m)
    res = run_bass_kernel_spmd(nc, in_maps, core_ids=list(range(n)))
    ys = [np.asarray(r["y"]) for r in res.results]
    y_prompt = np.stack([y[:2048] for y in ys], axis=0).astype(np.float32)
    y_sample = np.stack([y[2048:] for y in ys], axis=0).astype(np.float32)
    return (y_prompt, y_sample)
```

```python
import contextlib
import numpy as np
import ml_dtypes
import concourse.bass as bass
import concourse.mybir as mybir
from concourse.bass_utils import run_bass_kernel_spmd

F32 = mybir.dt.float32
BF16 = mybir.dt.bfloat16
ALU = mybir.AluOpType
AF = mybir.ActivationFunctionType
AX = mybir.AxisListType

D = 1024
DFF = 2816
NFC = DFF // 128
KC = 8
TT = 512
NSUB = 4
NH = 8
INW = 2560
CK = 31
RMS_EPS = 1e-6
LN_EPS = 1e-5
DILS = (1, 4, 16)
SB = 2048
MASKV = -30000.0


def sl_(start, n, d):
    return slice(start, start + (n - 1) * d + 1, d)


class Buf:
    __slots__ = ("ap", "w", "r")

    def __init__(self, ap=None):
        self.ap = ap
        self.w = None
        self.r = {}


class Prog:
    ENG = ("sync", "act", "dve", "pool", "pe")

    def __init__(self, nc, es, nds=32):
        self.nc = nc
        self.q = {e: [] for e in self.ENG}
        self.sem = {}
        self.cnt = {}
        for e in ("act", "dve", "pool", "pe"):
            self.sem[e] = es.enter_context(nc.semaphore("s_" + e))
            self.cnt[e] = 0
        self.NDS = nds
        self.NDP = 8
        for i in range(nds):
            k = "d%d" % i
            self.sem[k] = es.enter_context(nc.semaphore("s_" + k))
            self.cnt[k] = 0
        for i in range(self.NDP):
            k = "g%d" % i
            self.sem[k] = es.enter_context(nc.semaphore("s_" + k))
            self.cnt[k] = 0
        self.rrp = 0
        self.known = {e: {} for e in self.ENG}
        self.rr = 0
        self.pe_pending = False
        self.out_dma = []

    def _waits(self, eng, reads, writes, extra=()):
        w = {}

        def need(k, v):
            if v > w.get(k, 0):
                w[k] = v

        for b in reads:
            if b.w is not None:
                need(*b.w)
        for b in writes:
            if b.w is not None:
                need(*b.w)
            for k, v in b.r.items():
                need(k, v)
        for k, v in extra:
            need(k, v)
        out = []
        kn = self.known[eng]
        for k, v in w.items():
            if k == eng and eng == "pe":
                continue
            if kn.get(k, 0) >= v:
                continue
            kn[k] = v
            out.append((k, v))
        return out

    def op(self, eng, fn, reads=(), writes=(), sig=True):
        assert sig or eng == "pe"
        wl = self._waits(eng, reads, writes)
        if sig:
            self.cnt[eng] += 1
            tk = self.cnt[eng]
            if eng == "pe":
                self.pe_pending = False
        else:
            tk = self.cnt[eng] + 1
            self.pe_pending = True
        for b in reads:
            if b.r.get(eng, 0) < tk:
                b.r[eng] = tk
        for b in writes:
            b.w = (eng, tk)
            b.r = {}
        self.q[eng].append((wl, fn, True if sig else None))

    def dma(self, qeng, out_ap, in_ap, reads=(), writes=(), is_out=False):
        if qeng == "pool":
            i = self.rrp
            self.rrp = (i + 1) % self.NDP
            key = "g%d" % i
        else:
            i = self.rr
            self.rr = (i + 1) % self.NDS
            key = "d%d" % i
        prev = self.cnt[key]
        extra = ((key, prev),) if prev > 0 else ()
        wl = self._waits(qeng, reads, writes, extra)
        self.cnt[key] += 16
        tk = self.cnt[key]
        for b in reads:
            b.r[key] = tk
        for b in writes:
            b.w = (key, tk)
            b.r = {}
        self.q[qeng].append((wl, (lambda e, o=out_ap, a=in_ap: e.dma_start(out=o, in_=a)), key))

    def barrier(self):
        assert not self.pe_pending
        snap = {k: v for k, v in self.cnt.items() if v > 0}
        for e in self.ENG:
            wl = []
            kn = self.known[e]
            for k, v in snap.items():
                if k == e and e == "pe":
                    continue
                if kn.get(k, 0) >= v:
                    continue
                kn[k] = v
                wl.append((k, v))
            if wl:
                self.q[e].append((wl, None, None))

    def replay(self, block):
        m = {"sync": block.sync, "act": block.scalar, "dve": block.vector, "pool": block.gpsimd, "pe": block.tensor}
        for e in self.ENG:
            items = self.q[e]

            def body(engobj, items=items, e=e):
                for wl, fn, sig in items:
                    for k, v in wl:
                        engobj.wait_ge(self.sem[k], v)
                    if fn is None:
                        continue
                    ins = fn(engobj)
                    if sig is True:
                        ins.then_inc(self.sem[e], 1)
                    elif sig is not None:
                        ins.then_inc(self.sem[sig], 16)

            m[e](body)


class Arena:
    def __init__(self, ap):
        self.ap = ap
        self.n = ap.shape[1]
        self.off = 0

    def reset(self, off=0):
        self.off = off

    def f32(self, n):
        n4 = (n + 7) // 8 * 8
        assert self.off + n4 <= self.n, ("arena overflow", self.off, n4, self.n)
        v = self.ap[:, self.off:self.off + n]
        self.off += n4
        return v

    def bf16(self, n):
        n2 = (n + 1) // 2
        return self.f32(n2).bitcast(BF16)[:, 0:n]


def build(seqs=(2048, 8192), phases="ABCDE", debug=False):
    T = sum(seqs)
    NT = T // TT
    seq0 = [sum(seqs[:i]) for i in range(len(seqs))]
    nc = bass.Bass("TRN2", target_bir_lowering=False)
    okind = "ExternalOutput" if debug else "Internal"

    def din(name, shape, dt=F32):
        return nc.dram_tensor(name, list(shape), dt, kind="ExternalInput").ap()

    x_d = din("x", [T, D])
    wgu1_d = din("ffn1_w_gu", [D, 2 * DFF])
    wd1_d = din("ffn1_w_down", [DFF, D])
    wgu2_d = din("ffn2_w_gu", [D, 2 * DFF])
    wd2_d = din("ffn2_w_down", [DFF, D])
    win_d = din("w_in", [D, INW])
    wout_d = din("w_out", [D, D])
    gT_d = din("gT", [128, 3 * KC])
    gpost_d = din("gpost", [4, D])
    convw_d = din("convw_t", [512, CK])
    convv_d = din("convv", [128, 12])
    convb_d = din("conv_b", [512])
    ident_d = din("ident", [128, 128])
    bias_d = din("biasT", [128, 3 * NH * 256], BF16)
    y_d = nc.dram_tensor("y", [T, D], F32, kind="ExternalOutput").ap()
    X1 = nc.dram_tensor("X1", [T, D], F32, kind=okind).ap()
    X2 = nc.dram_tensor("X2", [T, D], F32, kind=okind).ap()
    Qs = nc.dram_tensor("Qs", [T, 512], BF16, kind=okind).ap()
    Ks = nc.dram_tensor("Ks", [T, 512], BF16, kind=okind).ap()
    Vs = nc.dram_tensor("Vs", [T, 512], BF16, kind=okind).ap()
    GluT = nc.dram_tensor("GluT", [512, T], BF16, kind=okind).ap()
    AttnT = nc.dram_tensor("AttnT", [512, T], BF16, kind=okind).ap()

    es = contextlib.ExitStack()
    with es:
        NAR = 52480
        arena_t = es.enter_context(nc.sbuf_tensor("arena", [128, NAR], F32))
        cst_t = es.enter_context(nc.sbuf_tensor("cst", [128, 640], F32))
        banks = [es.enter_context(nc.psum_tensor("bank%d" % i, [128, 512], F32)) for i in range(8)]
        pg = Prog(nc, es)
        AR = Arena(arena_t[:, :])
        CS = Arena(cst_t[:, :])

        identf = Buf(CS.f32(128))
        identb = Buf(CS.bf16(128))
        m05 = Buf(CS.f32(8))
        p05 = Buf(CS.f32(8))
        negM = Buf(CS.f32(8 * len(seqs)))
        onesf = Buf(CS.f32(128))
        pg.dma("sync", identf.ap, ident_d[:, :], writes=[identf])
        pg.op("dve", lambda e: e.tensor_copy(out=identb.ap, in_=identf.ap), reads=[identf], writes=[identb])
        pg.op("pool", lambda e: e.memset(m05.ap, -0.5), writes=[m05])
        pg.op("pool", lambda e: e.memset(p05.ap, 0.5), writes=[p05])
        pg.op("pool", lambda e: e.memset(onesf.ap, 1.0), writes=[onesf])
        pg.op("pool", lambda e: e.memset(negM.ap, 0.0), writes=[negM])

        bankf = [Buf(b[:, :]) for b in banks]

        def bbf(i):
            return banks[i][:, :].bitcast(BF16)

        def rstd_from(ss_buf, rstd_buf, v_buf, scale, eps):
            pg.op("pool", lambda e: e.tensor_scalar(out=v_buf.ap, in0=ss_buf.ap, scalar1=scale, scalar2=eps,
                                                    op0=ALU.mult, op1=ALU.add), reads=[ss_buf], writes=[v_buf])
            pg.op("pool", lambda e: e.tensor_tensor(out=rstd_buf.ap, in0=v_buf.ap, in1=m05.ap[:, 0:1], op=ALU.pow),
                  reads=[v_buf, m05], writes=[rstd_buf])

        def make_gbc(gbc, gT, col0):
            for kc in range(KC):
                pg.op("dve", lambda e, kc=kc: e.tensor_scalar(out=gbc.ap[:, kc * 128:(kc + 1) * 128], in0=onesf.ap,
                                                              scalar1=gT.ap[:, col0 + kc:col0 + kc + 1], scalar2=None,
                                                              op0=ALU.mult), reads=[onesf, gT], writes=[gbc])

        class Front:
            def __init__(self, src, gbc, psT_bank, nxin=2, nxh=4, nhT=1, alt_bank=None):
                self.src = src
                self.gbc = gbc
                self.bank = psT_bank
                self.xin = [Buf(AR.f32(D)) for _ in range(nxin)]
                self.xh = [Buf(AR.bf16(D)) for _ in range(nxh)]
                self.st = [[Buf(CS.f32(1)) for _ in range(3)] for _ in range(nxh)]
                self.hTs = [AR.bf16(KC * TT) for _ in range(nhT)]
                self.hTbs = [[Buf() for _ in range(NSUB)] for _ in range(nhT)]
                self.nhT = nhT
                self.alt_bank = alt_bank
                self.tc = 0
                self.hT = self.hTs[0]
                self.hTb = self.hTbs[0]
                self.nxin = nxin
                self.nxh = nxh
                self.ci = 0
                self.ch = 0
                self.pend = {}
                self.lpend = {}

            def load(self, t, s):
                xin = self.xin[self.ci % self.nxin]
                self.ci += 1
                r0 = t * TT + s * 128
                pg.dma("sync", xin.ap, self.src[r0:r0 + 128, :], writes=[xin])
                self.lpend[(t, s)] = xin

            def stats(self, t, s):
                if (t, s) not in self.lpend:
                    self.load(t, s)
                xin = self.lpend.pop((t, s))
                slot = self.ch % self.nxh
                self.ch += 1
                xh = self.xh[slot]
                ss, v, rs = self.st[slot]
                pg.op("act", lambda e: e.activation(out=xh.ap, in_=xin.ap, func=AF.Square, accum_out=ss.ap),
                      reads=[xin], writes=[xh, ss])
                rstd_from(ss, rs, v, 1.0 / D, RMS_EPS)
                pg.op("dve", lambda e: e.tensor_scalar(out=xh.ap, in0=xin.ap, scalar1=rs.ap, scalar2=None, op0=ALU.mult),
                      reads=[xin, rs], writes=[xh])
                self.pend[(t, s)] = xh

            def transp(self, t, s, bank=None):
                xh = self.pend.pop((t, s))
                if bank is None:
                    bank = self.bank
                    if self.alt_bank is not None and self.tc % 2 == 1:
                        bank = self.alt_bank
                self.tc += 1
                pb = bankf[bank]
                pv = bbf(bank)
                for kc in range(KC):
                    pg.op("pe", lambda e, kc=kc: e.transpose(out=pv[:, kc * 128:(kc + 1) * 128],
                                                             in_=xh.ap[:, kc * 128:(kc + 1) * 128], identity=identb.ap),
                          reads=[xh, identb], writes=[pb], sig=(kc == KC - 1))
                hT_ = self.hTs[t % self.nhT]
                hv = hT_.rearrange("p (k t) -> p k t", t=TT)[:, :, s * 128:(s + 1) * 128]
                pg.op("dve", lambda e: e.tensor_tensor(out=hv, in0=pv.rearrange("p (k t) -> p k t", t=128),
                                                       in1=self.gbc.ap.rearrange("p (k t) -> p k t", t=128), op=ALU.mult),
                      reads=[pb, self.gbc], writes=[self.hTbs[t % self.nhT][s]])

            def hk(self, kc, t=0):
                return self.hTs[t % self.nhT][:, kc * TT:(kc + 1) * TT]

            def sel(self, t):
                self.hT = self.hTs[t % self.nhT]
                self.hTb = self.hTbs[t % self.nhT]

        def load_weights_cast(dst_ap3, src2d, nk, ncols, buf, chunk):
            for k in range(nk):
                for c0 in range(0, ncols, chunk):
                    c1 = min(ncols, c0 + chunk)
                    pg.dma("pool", dst_ap3[:, k * ncols + c0:k * ncols + c1], src2d[k * 128:(k + 1) * 128, c0:c1], writes=[buf])

        def epilogue(psA, psB, res_src, r0, gpost, xr, stt, junk, dst, final_g=None, before=None, preloaded=False):
            ssA, ssB, v, rs, ss3, v3, rs3 = stt
            if not preloaded:
                pg.dma("sync", xr.ap, res_src[r0:r0 + 128, :], writes=[xr])
            if before is not None:
                before()
            jv = junk.ap
            pg.op("act", lambda e: e.activation(out=jv[:, 0:512], in_=psA.ap, func=AF.Square, accum_out=ssA.ap),
                  reads=[psA], writes=[junk, ssA])
            pg.op("act", lambda e: e.activation(out=jv[:, 512:1024], in_=psB.ap, func=AF.Square, accum_out=ssB.ap),
                  reads=[psB], writes=[junk, ssB])
            pg.op("pool", lambda e: e.tensor_tensor(out=ssA.ap, in0=ssA.ap, in1=ssB.ap, op=ALU.add), reads=[ssA, ssB], writes=[ssA])
            rstd_from(ssA, rs, v, 1.0 / D, RMS_EPS)
            for nh, ps in enumerate((psA, psB)):
                sl = slice(nh * 512, (nh + 1) * 512)
                pg.op("dve", lambda e, ps=ps, sl=sl: e.scalar_tensor_tensor(out=ps.ap, in0=ps.ap, scalar=rs.ap, in1=gpost.ap[:, sl],
                                                                           op0=ALU.mult, op1=ALU.mult),
                      reads=[ps, rs, gpost], writes=[ps])
                pg.op("dve", lambda e, ps=ps, sl=sl: e.tensor_tensor(out=xr.ap[:, sl], in0=ps.ap, in1=xr.ap[:, sl], op=ALU.add),
                      reads=[ps, xr], writes=[xr])
            def finish():
                if final_g is not None:
                    pg.op("act", lambda e: e.activation(out=jv, in_=xr.ap, func=AF.Square, accum_out=ss3.ap), reads=[xr], writes=[junk, ss3])
                    rstd_from(ss3, rs3, v3, 1.0 / D, RMS_EPS)
                    pg.op("dve", lambda e: e.scalar_tensor_tensor(out=xr.ap, in0=xr.ap, scalar=rs3.ap, in1=final_g.ap,
                                                                  op0=ALU.mult, op1=ALU.mult), reads=[xr, rs3, final_g], writes=[xr])
                pg.dma("sync", dst[r0:r0 + 128, :], xr.ap, reads=[xr])

            if final_g is None:
                finish()
                return None
            return finish

        def load_gpost(buf, row, scale):
            pg.dma("sync", buf.ap, gpost_d[row, :].partition_broadcast(128), writes=[buf])
            if scale != 1.0:
                pg.op("dve", lambda e: e.tensor_scalar(out=buf.ap, in0=buf.ap, scalar1=scale, scalar2=None, op0=ALU.mult),
                      reads=[buf], writes=[buf])

        def ffn_phase(src, wgu_d, wd_d, gcol, gpost_row, dst, final):
            AR.reset()
            CS.reset(cs_mark)
            Wgu = Buf(AR.bf16(KC * 2 * DFF))
            Wd = Buf(AR.bf16(NFC * D))
            PIECES = (1, 1, 2, 3, 4, 11)
            pstart = [sum(PIECES[:i]) for i in range(len(PIECES))]
            fc2p = []
            for i, n in enumerate(PIECES):
                fc2p += [i] * n
            WguB = [[Buf() for _ in PIECES] for _ in range(2)]
            WdB = [Buf() for _ in range(NFC)]
            Wgu3 = Wgu.ap.rearrange("p (k n) -> p k n", n=2 * DFF)
            wsrc3 = wgu_d.rearrange("(k p) n -> p k n", p=128)
            late = []
            for i, n in enumerate(PIECES):
                for gu in range(2):
                    c0 = gu * DFF + pstart[i] * 128
                    f_ = (lambda c0=c0, n=n, gu=gu, i=i: pg.dma("pool", Wgu3[:, :, c0:c0 + n * 128], wsrc3[:, :, c0:c0 + n * 128],
                                                                 writes=[WguB[gu][i]]))
                    if i == 0:
                        f_()
                    else:
                        late.append(f_)
            Wd3 = Wd.ap.rearrange("p (k n) -> p k n", n=D)
            wdsrc3 = wd_d.rearrange("(k p) n -> p k n", p=128)
            for (k0, k1) in ((0, 2), (2, 6), (6, 14), (14, 22)):
                late.append(lambda k0=k0, k1=k1: pg.dma("pool", Wd3[:, k0:k1, :], wdsrc3[:, k0:k1, :], writes=WdB[k0:k1]))
            gT = Buf(CS.f32(3 * KC))
            pg.dma("sync", gT.ap, gT_d[:, :], writes=[gT])
            gbc = Buf(AR.f32(KC * 128))
            make_gbc(gbc, gT, gcol * KC)
            gpost = Buf(AR.f32(D))
            load_gpost(gpost, gpost_row, 0.5)
            gfin = None
            if final:
                gfin = Buf(AR.f32(D))
                load_gpost(gfin, 3, 1.0)
            fr = Front(src, gbc, 0)
            hid = AR.bf16(NFC * TT)
            hidb = [Buf() for _ in range(NFC)]
            sg = [Buf(AR.f32(TT)) for _ in range(2)]
            xr = [Buf(AR.f32(D)) for _ in range(2)]
            junk = Buf(AR.bf16(D))
            stts = [[Buf(CS.f32(1)) for _ in range(7)] for _ in range(2)]
            guring = [bankf[1], bankf[2], bankf[3]]
            gctr = 0
            psD = [bankf[4], bankf[5], bankf[6], bankf[7]]
            nd = 0
            for s in range(NSUB):
                fr.stats(0, s)
            for f_ in late:
                f_()
            for s in range(NSUB):
                fr.transp(0, s)
            ne = 0
            pend_fin = [None]
            for t in range(NT):
                for fc in range(NFC):
                    if fc == 3 and pend_fin[0] is not None:
                        pend_fin[0]()
                        pend_fin[0] = None
                    if t + 1 < NT and fc in (1, 5, 9, 13):
                        fr.load(t + 1, fc // 4)
                    if t + 1 < NT and fc in (4, 8, 12, 16):
                        fr.stats(t + 1, fc // 4 - 1)
                    g, u = guring[gctr % 3], guring[(gctr + 1) % 3]
                    gctr += 2
                    for kc in range(KC):
                        pg.op("pe", lambda e, kc=kc, fc=fc, g=g: e.matmul(
                            g.ap, lhsT=Wgu.ap[:, kc * 2 * DFF + fc * 128: kc * 2 * DFF + (fc + 1) * 128],
                            rhs=fr.hk(kc), start=(kc == 0), stop=(kc == KC - 1)),
                            reads=[WguB[0][fc2p[fc]]] + fr.hTb, writes=[g], sig=(kc == KC - 1))
                    for kc in range(KC):
                        pg.op("pe", lambda e, kc=kc, fc=fc, u=u: e.matmul(
                            u.ap, lhsT=Wgu.ap[:, kc * 2 * DFF + DFF + fc * 128: kc * 2 * DFF + DFF + (fc + 1) * 128],
                            rhs=fr.hk(kc), start=(kc == 0), stop=(kc == KC - 1)),
                            reads=[WguB[1][fc2p[fc]]] + fr.hTb, writes=[u], sig=(kc == KC - 1))
                    sgb = sg[fc % 2]
                    pg.op("act", lambda e, g=g, sgb=sgb: e.activation(out=sgb.ap, in_=g.ap, func=AF.Silu), reads=[g], writes=[sgb])
                    pg.op("dve", lambda e, u=u, sgb=sgb, fc=fc: e.tensor_tensor(out=hid[:, fc * TT:(fc + 1) * TT], in0=u.ap, in1=sgb.ap,
                                                                                op=ALU.mult), reads=[u, sgb], writes=[hidb[fc]])
                if t + 1 < NT:
                    for s in range(NSUB):
                        if s % 2 == 0:
                            fr.transp(t + 1, s)
                        else:
                            fr.transp(t + 1, s, bank=1 + gctr % 3)
                            gctr += 1
                for s in range(NSUB):
                    pss = []
                    for nh in range(2):
                        pd = psD[nd % 4]
                        nd += 1
                        pss.append(pd)
                        for fc in range(NFC):
                            pg.op("pe", lambda e, fc=fc, s=s, nh=nh, pd=pd: e.matmul(
                                pd.ap, lhsT=hid[:, fc * TT + s * 128: fc * TT + (s + 1) * 128],
                                rhs=Wd.ap[:, fc * D + nh * 512: fc * D + (nh + 1) * 512], start=(fc == 0), stop=(fc == NFC - 1)),
                                reads=[WdB[fc]] + hidb, writes=[pd], sig=(fc == NFC - 1))
                    pend_fin[0] = epilogue(pss[0], pss[1], src, t * TT + s * 128, gpost, xr[ne % 2], stts[ne % 2], junk, dst, gfin,
                                           before=pend_fin[0])
                    ne += 1
            if pend_fin[0] is not None:
                pend_fin[0]()
            pg.barrier()

        def phase_b():
            AR.reset()
            CS.reset(cs_mark)
            Win = Buf(AR.bf16(KC * INW))
            WinB = [Buf() for _ in range(5)]
            Win3 = Win.ap.rearrange("p (k n) -> p k n", n=INW)
            winsrc3 = win_d.rearrange("(k p) n -> p k n", p=128)
            for blk in range(5):
                pg.dma("pool", Win3[:, :, blk * 512:(blk + 1) * 512], winsrc3[:, :, blk * 512:(blk + 1) * 512], writes=[WinB[blk]])
            gT = Buf(CS.f32(3 * KC))
            pg.dma("sync", gT.ap, gT_d[:, :], writes=[gT])
            gbc = Buf(AR.f32(KC * 128))
            make_gbc(gbc, gT, 1 * KC)
            fr = Front(X1, gbc, 0, nhT=2, alt_bank=7)
            qst = [Buf(AR.bf16(512)) for _ in range(2)]
            kst = [Buf(AR.bf16(512)) for _ in range(2)]
            vst = [Buf(AR.bf16(512)) for _ in range(2)]
            sq = [Buf(AR.f32(512)) for _ in range(2)]
            sgm = [Buf(AR.f32(512)) for _ in range(2)]
            glst = [Buf(AR.bf16(4 * TT)) for _ in range(2)]
            nrm = [Buf(CS.f32(8)) for _ in range(4)]
            RM = [Buf(CS.f32(8)), Buf(CS.f32(8))]
            mx = Buf(CS.f32(2))
            m2 = Buf(CS.f32(1))
            Mv = Buf(CS.f32(1))
            dg = Buf(CS.f32(8))
            nm8 = Buf(CS.f32(8))
            ring = [bankf[i] for i in range(1, 7)]
            rc = [0]

            def nxt():
                b = ring[rc[0] % len(ring)]
                rc[0] += 1
                return b

            cq = 0
            for s in range(NSUB):
                fr.stats(0, s)
            for s in range(NSUB):
                fr.transp(0, s)
            for si, S in enumerate(seqs):
                for rm in RM:
                    pg.op("pool", lambda e, rm=rm: e.memset(rm.ap, 0.0), writes=[rm])
                for tl in range(S // TT):
                    t = seq0[si] // TT + tl
                    fr.sel(t)
                    for s in range(NSUB):
                        if t + 1 < NT:
                            fr.load(t + 1, s)
                        r0 = t * TT + s * 128
                        outs = []
                        for blk in range(3):
                            ps = nxt()
                            outs.append(ps)
                            for kc in range(KC):
                                pg.op("pe", lambda e, kc=kc, s=s, blk=blk, ps=ps, hT_=fr.hT: e.matmul(
                                    ps.ap, lhsT=hT_[:, kc * TT + s * 128: kc * TT + (s + 1) * 128],
                                    rhs=Win.ap[:, kc * INW + blk * 512: kc * INW + (blk + 1) * 512],
                                    start=(kc == 0), stop=(kc == KC - 1)), reads=[WinB[blk], fr.hTb[s]], writes=[ps], sig=(kc == KC - 1))
                        for wi, (ps, stg, scale, dstT) in enumerate(((outs[0], qst[cq % 2], 0.125, Qs), (outs[1], kst[cq % 2], 1.0, Ks))):
                            sqb = sq[wi]
                            nb = nrm[(cq % 2) * 2 + wi]
                            pg.op("act", lambda e, ps=ps, stg=stg, scale=scale: e.activation(out=stg.ap, in_=ps.ap, func=AF.Copy, scale=scale),
                                  reads=[ps], writes=[stg])
                            pg.op("act", lambda e, stg=stg, sqb=sqb: e.activation(out=sqb.ap, in_=stg.ap, func=AF.Square), reads=[stg], writes=[sqb])
                            pg.op("dve", lambda e, sqb=sqb, nb=nb: e.tensor_reduce(out=nb.ap, in_=sqb.ap.rearrange("p (h d) -> p h d", d=64),
                                                                                   axis=AX.X, op=ALU.add), reads=[sqb], writes=[nb])
                            pg.op("dve", lambda e, nb=nb, wi=wi: e.tensor_tensor(out=RM[wi].ap, in0=RM[wi].ap, in1=nb.ap, op=ALU.max),
                                  reads=[nb, RM[wi]], writes=[RM[wi]])
                            pg.dma("sync", dstT[r0:r0 + 128, :], stg.ap, reads=[stg])
                        vb = vst[cq % 2]
                        pg.op("dve", lambda e, vb=vb, ps=outs[2]: e.tensor_copy(out=vb.ap, in_=ps.ap), reads=[outs[2]], writes=[vb])
                        pg.dma("sync", Vs[r0:r0 + 128, :], vb.ap, reads=[vb])
                        cq += 1
                        if t + 1 < NT and s >= 1:
                            fr.stats(t + 1, s - 1)
                    gl = glst[tl % 2]
                    for c in range(4):
                        if c == 1 and t + 1 < NT:
                            fr.stats(t + 1, 3)
                        if c == 3 and t + 1 < NT:
                            for s in range(NSUB):
                                fr.transp(t + 1, s)
                        pa, pb = nxt(), nxt()
                        for (ps, col) in ((pa, 1536 + c * 128), (pb, 2048 + c * 128)):
                            for kc in range(KC):
                                pg.op("pe", lambda e, kc=kc, ps=ps, col=col, t=t: e.matmul(
                                    ps.ap, lhsT=Win.ap[:, kc * INW + col: kc * INW + col + 128], rhs=fr.hk(kc, t),
                                    start=(kc == 0), stop=(kc == KC - 1)), reads=[WinB[col // 512]] + fr.hTb, writes=[ps], sig=(kc == KC - 1))
                        sb_ = sgm[c % 2]
                        pg.op("act", lambda e, pb=pb, sb_=sb_: e.activation(out=sb_.ap, in_=pb.ap, func=AF.Sigmoid), reads=[pb], writes=[sb_])
                        pg.op("dve", lambda e, pa=pa, sb_=sb_, gl=gl, c=c: e.tensor_tensor(out=gl.ap[:, c * TT:(c + 1) * TT], in0=pa.ap, in1=sb_.ap,
                                                                                          op=ALU.mult), reads=[pa, sb_], writes=[gl])
                    pg.dma("sync", GluT.rearrange("(c p) t -> p c t", p=128)[:, :, t * TT:(t + 1) * TT],
                           gl.ap.rearrange("p (c t) -> p c t", t=TT), reads=[gl])
                px = nxt()
                for wi in range(2):
                    pg.op("pe", lambda e, wi=wi, px=px: e.transpose(out=px.ap[0:8, wi * 128:(wi + 1) * 128], in_=RM[wi].ap, identity=identf.ap),
                          reads=[RM[wi], identf], writes=[px], sig=True)
                pg.op("dve", lambda e, px=px: e.tensor_reduce(out=mx.ap[0:8, :], in_=px.ap[0:8, 0:256].rearrange("p (a b) -> p a b", b=128),
                                                              axis=AX.X, op=ALU.max), reads=[px], writes=[mx])
                pg.op("dve", lambda e: e.tensor_tensor(out=m2.ap[0:8, :], in0=mx.ap[0:8, 0:1], in1=mx.ap[0:8, 1:2], op=ALU.mult),
                      reads=[mx], writes=[m2])
                pg.op("pool", lambda e: e.tensor_tensor(out=Mv.ap[0:8, :], in0=m2.ap[0:8, :], in1=p05.ap[0:8, 0:1], op=ALU.pow),
                      reads=[m2, p05], writes=[Mv])
                pg.op("dve", lambda e: e.tensor_scalar(out=dg.ap[0:8, :], in0=identf.ap[0:8, 0:8], scalar1=Mv.ap[0:8, :], scalar2=-1.001,
                                                       op0=ALU.mult, op1=ALU.mult), reads=[identf, Mv], writes=[dg])
                px2 = nxt()
                pg.op("pe", lambda e, px2=px2: e.matmul(px2.ap[:, 0:8], lhsT=onesf.ap[0:8, :], rhs=dg.ap[0:8, :], start=True, stop=True),
                      reads=[onesf, dg], writes=[px2], sig=True)
                pg.op("dve", lambda e, px2=px2: e.tensor_copy(out=nm8.ap, in_=px2.ap[:, 0:8]), reads=[px2], writes=[nm8])
                pg.op("dve", lambda e, si=si: e.tensor_tensor(out=negM.ap[:, si * 4:(si + 1) * 4], in0=nm8.ap[:, 0:8:2], in1=nm8.ap[:, 1:8:2],
                                                              op=ALU.min), reads=[nm8], writes=[negM])
            pg.barrier()

        def phase_c():
            AR.reset()
            CS.reset(cs_mark)
            Accs = [AR.f32(NH * SB) for _ in range(2)]
            Acc3s = [a_.rearrange("p (h t) -> p h t", t=SB) for a_ in Accs]
            accbs = [[Buf() for _ in range(4)] for _ in range(2)]
            biasb = Buf(AR.bf16(3 * NH * 256))
            pg.dma("sync", biasb.ap, bias_d[:, :], writes=[biasb])
            aouts = [Buf(AR.bf16(SB)) for _ in range(2)]
            Rbs = [Buf(AR.f32(SB)) for _ in range(2)]
            Rbh = [[Buf(), Buf()] for _ in range(2)]
            qland = [Buf(AR.bf16(512)) for _ in range(2)]
            kland = [Buf(AR.bf16(512)) for _ in range(4)]
            vland = [Buf(AR.bf16(512)) for _ in range(4)]
            qT = [Buf(AR.bf16(1024)) for _ in range(2)]
            pT = [Buf(AR.bf16(512)) for _ in range(3)]

            class Slot:
                def __init__(self):
                    self.kT = Buf(AR.bf16(512))
                    self.va = Buf(AR.bf16(1024))

            slots = {"N": [Slot() for _ in range(4)], "F": [Slot() for _ in range(2)], "L": [Slot() for _ in range(2)]}
            sctr = {"N": 0, "F": 0, "L": 0}
            for b in qT + kland + qland + vland:
                pg.op("pool", lambda e, b=b: e.memset(b.ap, 0.0), writes=[b])
            for sl in slots["N"]:
                pg.op("pool", lambda e, sl=sl: e.memset(sl.va.ap, 1.0), writes=[sl.va])
            for typ, (vlo, vhi) in (("F", (64, 128)), ("L", (0, 64))):
                for sl in slots[typ]:
                    pg.op("pool", lambda e, sl=sl: e.memset(sl.va.ap, 0.0), writes=[sl.va])
                    pg.op("pool", lambda e, sl=sl: e.memset(sl.kT.ap, 0.0), writes=[sl.kT])
                    v3 = sl.va.ap.rearrange("p (h d) -> p h d", d=128)
                    pg.op("dve", lambda e, v3=v3, vlo=vlo, vhi=vhi: e.memset(v3[vlo:vhi, 0:8:2, 64:128], 1.0), writes=[sl.va])
                    pg.op("dve", lambda e, v3=v3, vlo=vlo, vhi=vhi: e.memset(v3[vlo:vhi, 1:8:2, 0:64], 1.0), writes=[sl.va])
            psT = [0, 1]
            psS = [bankf[2], bankf[3], bankf[4]]
            psO = [bankf[5], bankf[6], bankf[7]]
            tctr = [0]

            tiles = []
            keys = []
            sbg = -1
            for si, S in enumerate(seqs):
                for sb in range(S // SB):
                    b0 = sb * SB
                    sbg += 1
                    for di, d in enumerate(DILS):
                        L = S // d
                        nt = SB // d // 128
                        for r in range(d):
                            base = len(keys)
                            for j in range(nt + 1):
                                Uk = b0 // d + 128 * j - 64
                                lo = max(0, -Uk)
                                hi = min(128, L - Uk)
                                typ = "F" if lo > 0 else ("L" if hi < 128 else "N")
                                keys.append(dict(tk=seq0[si] + r + d * (Uk + lo), lo=lo, hi=hi, typ=typ, d=d))
                            for i in range(nt):
                                tiles.append(dict(si=si, sbk=(si, sb), sbg=sbg, di=di, d=d, r=r, i=i, k0=base + i, k1=base + i + 1,
                                                  tq=seq0[si] + b0 + r + d * 128 * i, tb=seq0[si] + b0))
            NTL = len(tiles)
            lctr = [0]

            def load_key(m):
                k = keys[m]
                if "land" in k:
                    return
                li = lctr[0] % 4
                lctr[0] += 1
                k["land"] = li
                kl, vl = kland[li], vland[li]
                lo, hi, d = k["lo"], k["hi"], k["d"]
                n = hi - lo
                if n < 128:
                    zlo = 0 if lo > 0 else hi
                    pg.op("pool", lambda e: e.memset(kl.ap[zlo:zlo + 64, :], 0.0), writes=[kl])
                pg.dma("sync", kl.ap[lo:hi, :], Ks[sl_(k["tk"], n, d), :], writes=[kl])
                pg.dma("sync", vl.ap[lo:hi, :], Vs[sl_(k["tk"], n, d), :], writes=[vl])

            def tr_key(m):
                k = keys[m]
                if "slot" in k:
                    return
                typ = k["typ"]
                sl = slots[typ][sctr[typ] % len(slots[typ])]
                sctr[typ] += 1
                k["slot"] = sl
                kl, vl = kland[k["land"]], vland[k["land"]]
                bi = psT[tctr[0] % 2]
                tctr[0] += 1
                pb, pv = bankf[bi], bbf(bi)
                for c in range(4):
                    pg.op("pe", lambda e, c=c: e.transpose(out=pv[:, c * 128:(c + 1) * 128], in_=kl.ap[:, c * 128:(c + 1) * 128],
                                                           identity=identb.ap), reads=[kl, identb], writes=[pb], sig=(c == 3))
                if tctr[0] % 4 < 2:
                    pg.op("act", lambda e: e.activation(out=sl.kT.ap, in_=pv[:, 0:512], func=AF.Copy), reads=[pb], writes=[sl.kT])
                else:
                    pg.op("dve", lambda e: e.tensor_copy(out=sl.kT.ap, in_=pv[:, 0:512]), reads=[pb], writes=[sl.kT])
                lo, hi = k["lo"], k["hi"]
                v3 = sl.va.ap.rearrange("p (h d) -> p h d", d=128)
                l3 = vl.ap.rearrange("p (h d) -> p h d", d=64)
                if hi - lo == 128:
                    pg.op("pool", lambda e: e.tensor_copy(out=v3[:, 0:8:2, 0:64], in_=l3[:, 0:8:2, :]), reads=[vl], writes=[sl.va])
                    pg.op("pool", lambda e: e.tensor_copy(out=v3[:, 1:8:2, 64:128], in_=l3[:, 1:8:2, :]), reads=[vl], writes=[sl.va])
                else:
                    pg.op("act", lambda e: e.activation(out=v3[lo:hi, 0:8:2, 0:64], in_=l3[lo:hi, 0:8:2, :], func=AF.Copy), reads=[vl], writes=[sl.va])
                    pg.op("act", lambda e: e.activation(out=v3[lo:hi, 1:8:2, 64:128], in_=l3[lo:hi, 1:8:2, :], func=AF.Copy), reads=[vl], writes=[sl.va])

            def load_tile(n):
                tl = tiles[n]
                qb = qland[n % 2]
                pg.dma("sync", qb.ap, Qs[sl_(tl["tq"], 128, tl["d"]), :], writes=[qb])
                load_key(tl["k0"])
                load_key(tl["k1"])

            def tr_tile(n):
                tl = tiles[n]
                qb = qland[n % 2]
                qtb = qT[n % 2]
                bi = psT[tctr[0] % 2]
                tctr[0] += 1
                pb, pv = bankf[bi], bbf(bi)
                for c in range(4):
                    pg.op("pe", lambda e, c=c: e.transpose(out=pv[:, c * 128:(c + 1) * 128], in_=qb.ap[:, c * 128:(c + 1) * 128],
                                                           identity=identb.ap), reads=[qb, identb], writes=[pb], sig=(c == 3))
                for hh in range(2):
                    pg.op("dve", lambda e, hh=hh: e.tensor_copy(
                        out=qtb.ap[64 * hh:64 * hh + 64, :].rearrange("p (c v q) -> p c v q", v=2, q=128)[:, :, hh, :],
                        in_=pv[64 * hh:64 * hh + 64, 0:512].rearrange("p (c q) -> p c q", q=128)), reads=[pb], writes=[qtb])
                tr_key(tl["k0"])
                tr_key(tl["k1"])

            P = [(n, hp) for n in range(NTL) for hp in range(4)]

            def emit_S(idx):
                n, hp = P[idx]
                tl = tiles[n]
                qtb = qT[n % 2]
                ks = (keys[tl["k0"]]["slot"], keys[tl["k1"]]["slot"])
                ps, pt = psS[idx % 3], pT[idx % 3]
                di, si = tl["di"], tl["si"]
                for hh in range(2):
                    h = 2 * hp + hh
                    bo = (di * NH + h) * 256
                    pg.op("pe", lambda e, bo=bo, hh=hh: e.matmul(ps.ap[:, hh * 256:(hh + 1) * 256], lhsT=identb.ap,
                                                                 rhs=biasb.ap[:, bo:bo + 256], start=True, stop=False),
                          reads=[identb, biasb], writes=[ps], sig=False)
                    for kt in range(2):
                        pg.op("pe", lambda e, kt=kt, hh=hh: e.matmul(
                            ps.ap[:, hh * 256 + kt * 128: hh * 256 + (kt + 1) * 128],
                            lhsT=ks[kt].kT.ap[:, hp * 128:(hp + 1) * 128],
                            rhs=qtb.ap[:, (hp * 2 + hh) * 128:(hp * 2 + hh + 1) * 128], start=False, stop=(kt == 1)),
                            reads=[ks[kt].kT, qtb], writes=[ps], sig=(kt == 1 and hh == 1))
                pg.op("act", lambda e: e.activation(out=pt.ap, in_=ps.ap, func=AF.Exp, bias=negM.ap[:, si * 4 + hp: si * 4 + hp + 1]),
                      reads=[ps, negM], writes=[pt])

            def emit_PV(idx):
                n, hp = P[idx]
                tl = tiles[n]
                ks = (keys[tl["k0"]]["slot"], keys[tl["k1"]]["slot"])
                pt, po = pT[idx % 3], psO[idx % 3]
                d, r, i, di = tl["d"], tl["r"], tl["i"], tl["di"]
                for hh in range(2):
                    h = 2 * hp + hh
                    for kt in range(2):
                        pg.op("pe", lambda e, kt=kt, hh=hh, h=h: e.matmul(
                            po.ap[:, hh * 128:(hh + 1) * 128], lhsT=ks[kt].va.ap[:, h * 128:(h + 1) * 128],
                            rhs=pt.ap[:, hh * 256 + kt * 128: hh * 256 + (kt + 1) * 128], start=(kt == 0), stop=(kt == 1)),
                            reads=[ks[kt].va, pt], writes=[po], sig=(kt == 1 and hh == 1))
                Acc3 = Acc3s[tl["sbg"] % 2]
                accb = accbs[tl["sbg"] % 2]
                av = Acc3[:, 2 * hp:2 * hp + 2, sl_(r + d * 128 * i, 128, d)]
                pv_ = po.ap[:, 0:256].rearrange("p (h t) -> p h t", t=128)
                if di == 0:
                    pg.op("dve", lambda e: e.tensor_copy(out=av, in_=pv_), reads=[po], writes=[accb[hp]])
                else:
                    pg.op("dve", lambda e: e.tensor_tensor(out=av, in0=pv_, in1=av, op=ALU.add), reads=[po, accb[hp]], writes=[accb[hp]])

            def norm_steps(tl, h):
                Acc = Accs[tl["sbg"] % 2]
                accb = accbs[tl["sbg"] % 2]
                c = h // 2
                nlo, dlo = (0, 64) if h % 2 == 0 else (64, 0)
                Av = Acc[:, h * SB:(h + 1) * SB]
                Rb = Rbs[h % 2]
                ao = aouts[c % 2]
                HS = SB // 2
                steps = []
                for hf in range(2):
                    cs = slice(hf * HS, (hf + 1) * HS)
                    rbb = Rbh[h % 2][hf]
                    steps.append(lambda cs=cs, rbb=rbb: pg.op("act", lambda e: e.activation(out=Rb.ap[nlo:nlo + 64, cs], in_=Av[dlo:dlo + 64, cs], func=AF.Ln),
                                                              reads=[accb[c]], writes=[rbb]))
                    steps.append(lambda cs=cs, rbb=rbb: pg.op("act", lambda e: e.activation(out=Rb.ap[nlo:nlo + 64, cs], in_=Rb.ap[nlo:nlo + 64, cs],
                                                                                           func=AF.Exp, scale=-1.0), reads=[rbb], writes=[rbb]))
                    steps.append(lambda cs=cs, rbb=rbb: pg.op("dve", lambda e: e.tensor_tensor(out=ao.ap[nlo:nlo + 64, cs], in0=Av[nlo:nlo + 64, cs],
                                                                                              in1=Rb.ap[nlo:nlo + 64, cs], op=ALU.mult),
                                                              reads=[accb[c], rbb], writes=[ao]))
                if h % 2 == 1:
                    tb = tl["tb"]
                    steps.append(lambda: pg.dma("sync", AttnT[c * 128:(c + 1) * 128, tb:tb + SB], ao.ap, reads=[ao]))
                return steps

            pend_norm = []
            load_tile(0)
            if NTL > 1:
                load_tile(1)
            tr_tile(0)
            emit_S(0)
            emit_S(1)
            for idx, (n, hp) in enumerate(P):
                if hp == 0:
                    if n + 1 < NTL:
                        tr_tile(n + 1)
                    if n + 2 < NTL:
                        load_tile(n + 2)
                if idx + 2 < len(P):
                    emit_S(idx + 2)
                emit_PV(idx)
                if pend_norm:
                    pend_norm.pop(0)()
                if hp == 3 and (n + 1 == NTL or tiles[n + 1]["sbk"] != tiles[n]["sbk"]):
                    while pend_norm:
                        pend_norm.pop(0)()
                    for h in range(NH):
                        pend_norm.extend(norm_steps(tiles[n], h))
            while pend_norm:
                pend_norm.pop(0)()
            pg.barrier()

        def phase_d():
            AR.reset()
            CS.reset(cs_mark)
            Wout = Buf(AR.bf16(KC * D))
            cwT = Buf(AR.f32(4 * CK))
            pg.dma("sync", cwT.ap.rearrange("p (c j) -> p c j", j=CK), convw_d.rearrange("(c p) j -> p c j", p=128), writes=[cwT])
            diag = Buf(AR.bf16(CK * 4 * 128))
            diagb = [Buf() for _ in range(CK * 4)]
            for j in range(CK):
                for c in range(4):
                    o = (j * 4 + c) * 128
                    if (j * 4 + c) % 2 == 0:
                        pg.op("dve", lambda e, o=o, c=c, j=j: e.tensor_scalar(out=diag.ap[:, o:o + 128], in0=identf.ap,
                                                                              scalar1=cwT.ap[:, c * CK + j: c * CK + j + 1], scalar2=None,
                                                                              op0=ALU.mult), reads=[identf, cwT], writes=[diagb[j * 4 + c]])
                    else:
                        pg.op("act", lambda e, o=o, c=c, j=j: e.activation(out=diag.ap[:, o:o + 128], in_=identf.ap, func=AF.Copy,
                                                                           scale=cwT.ap[:, c * CK + j: c * CK + j + 1]),
                              reads=[identf, cwT], writes=[diagb[j * 4 + c]])
            cv = Buf(CS.f32(12))
            pg.dma("sync", cv.ap, convv_d[:, :], writes=[cv])
            ones2 = Buf(CS.f32(2))
            pg.op("pool", lambda e: e.memset(ones2.ap, 1.0), writes=[ones2])
            gpost = Buf(AR.f32(D))
            load_gpost(gpost, 1, 1.0)
            WN = TT + CK - 1
            gw = [Buf(AR.bf16(4 * WN + 2)) for _ in range(3)]
            at = [Buf(AR.bf16(4 * TT)) for _ in range(4)]
            loaded_d = set()
            cT = [AR.f32(4 * TT) for _ in range(2)]
            cTb = [[Buf() for _ in range(4)] for _ in range(2)]
            csq = [AR.f32(4 * TT) for _ in range(2)]
            csqb = [[Buf() for _ in range(4)] for _ in range(2)]
            convT = [AR.bf16(4 * TT) for _ in range(2)]
            convTb = [[Buf() for _ in range(4)] for _ in range(2)]
            dR = [Buf(AR.f32(128)) for _ in range(2)]
            dS = [Buf(AR.f32(128)) for _ in range(2)]
            sm = [[Buf(CS.f32(4)) for _ in range(6)] for _ in range(2)]
            xr = [Buf(AR.f32(D)) for _ in range(3)]
            junk = Buf(AR.bf16(D))
            stts = [[Buf(CS.f32(1)) for _ in range(7)] for _ in range(2)]
            xr_issued = set()

            def xr_load(e):
                if e in xr_issued or e >= NTD * NSUB:
                    return
                xr_issued.add(e)
                si_, tl_ = tl_list[e // NSUB]
                r0_ = seq0[si_] + tl_ * TT + (e % NSUB) * 128
                pg.dma("sync", xr[e % 3].ap, X1[r0_:r0_ + 128, :], writes=[xr[e % 3]])

            psC = [bankf[0], bankf[1]]
            bcR, bcS = bankf[2], bankf[3]
            pst = bcS
            psO = [bankf[4], bankf[5], bankf[6], bankf[7]]
            tl_list = []
            for si, S in enumerate(seqs):
                for tl in range(S // TT):
                    tl_list.append((si, tl))
            NTD = len(tl_list)
            GT = GluT.rearrange("(c p) t -> p c t", p=128)
            AT3 = AttnT.rearrange("(c p) t -> p c t", p=128)
            cc = [0]
            oc = [0]
            ec = [0]

            def dload(t):
                if t in loaded_d or t >= NTD:
                    return
                loaded_d.add(t)
                si, tl = tl_list[t]
                S = seqs[si]
                t0 = seq0[si] + tl * TT
                g = gw[t % 3]
                gv = g.ap[:, 0:4 * WN].rearrange("p (c w) -> p c w", w=WN)
                lo, hi = 0, WN
                if tl == 0:
                    lo = 15
                    pg.op("pool", lambda e: e.memset(gv[:, :, 0:15], 0.0), writes=[g])
                if tl == S // TT - 1:
                    hi = WN - 15
                    pg.op("pool", lambda e: e.memset(gv[:, :, WN - 15:WN], 0.0), writes=[g])
                pg.dma("sync", gv[:, :, lo:hi], GT[:, :, t0 - 15 + lo:t0 - 15 + hi], writes=[g])
                a = at[t % 4]
                pg.dma("sync", a.ap.rearrange("p (c t) -> p c t", t=TT), AT3[:, :, t0:t0 + TT], writes=[a])

            def conv(t):
                dload(t)
                dload(t + 1)
                g = gw[t % 3]
                k = t % 2
                for c in range(4):
                    pc = psC[cc[0] % 2]
                    cc[0] += 1
                    for j in range(CK):
                        o = (j * 4 + c) * 128
                        pg.op("pe", lambda e, c=c, j=j, o=o, pc=pc: e.matmul(
                            pc.ap, lhsT=diag.ap[:, o:o + 128], rhs=g.ap[:, c * WN + j: c * WN + j + TT],
                            start=(j == 0), stop=(j == CK - 1)), reads=[g, diagb[j * 4 + c]], writes=[pc], sig=(j == CK - 1))
                    pg.op("act", lambda e, c=c, pc=pc: e.activation(out=cT[k][:, c * TT:(c + 1) * TT], in_=pc.ap, func=AF.Identity,
                                                                    bias=cv.ap[:, 8 + c:9 + c]), reads=[pc, cv], writes=[cTb[k][c]])
                    pg.op("act", lambda e, c=c, pc=pc: e.activation(out=csq[k][:, c * TT:(c + 1) * TT], in_=pc.ap, func=AF.Square,
                                                                    bias=cv.ap[:, 8 + c:9 + c]), reads=[pc, cv], writes=[csqb[k][c]])

            def stats(t):
                k = t % 2
                for s in range(NSUB):
                    for gi, (src, srcb) in enumerate(((cT[k], cTb[k]), (csq[k], csqb[k]))):
                        for c in range(4):
                            pg.op("pe", lambda e, c=c, s=s, gi=gi, src=src: e.matmul(
                                pst.ap[:, gi * 8 + s * 2: gi * 8 + s * 2 + 2], lhsT=src[:, c * TT + s * 128: c * TT + (s + 1) * 128],
                                rhs=ones2.ap, start=(c == 0), stop=(c == 3)), reads=[srcb[c], ones2], writes=[pst],
                                sig=(c == 3 and gi == 1 and s == NSUB - 1))
                mean, msq, var, vv, rstd, shift = sm[k]
                pg.op("dve", lambda e: e.tensor_scalar(out=mean.ap, in0=pst.ap[:, 0:8:2], scalar1=1.0 / 512, scalar2=None, op0=ALU.mult),
                      reads=[pst], writes=[mean])
                pg.op("dve", lambda e: e.tensor_tensor(out=msq.ap, in0=mean.ap, in1=mean.ap, op=ALU.mult), reads=[mean], writes=[msq])
                pg.op("dve", lambda e: e.scalar_tensor_tensor(out=var.ap, in0=pst.ap[:, 8:16:2], scalar=1.0 / 512, in1=msq.ap,
                                                              op0=ALU.mult, op1=ALU.subtract), reads=[pst, msq], writes=[var])
                pg.op("pool", lambda e: e.tensor_scalar(out=vv.ap, in0=var.ap, scalar1=1.0, scalar2=LN_EPS, op0=ALU.mult, op1=ALU.add),
                      reads=[var], writes=[vv])
                pg.op("pool", lambda e: e.tensor_tensor(out=rstd.ap, in0=vv.ap, in1=m05.ap[:, 0:4], op=ALU.pow), reads=[vv, m05], writes=[rstd])
                pg.op("dve", lambda e: e.scalar_tensor_tensor(out=shift.ap, in0=mean.ap, scalar=-1.0, in1=rstd.ap, op0=ALU.mult, op1=ALU.mult),
                      reads=[mean, rstd], writes=[shift])

            def bc_norm(t):
                k = t % 2
                mean, msq, var, vv, rstd, shift = sm[k]
                for s in range(NSUB):
                    r_, s_ = dR[s % 2], dS[s % 2]
                    pg.op("dve", lambda e, s=s, r_=r_: e.tensor_scalar(out=r_.ap, in0=identf.ap, scalar1=rstd.ap[:, s:s + 1], scalar2=None, op0=ALU.mult),
                          reads=[identf, rstd], writes=[r_])
                    pg.op("dve", lambda e, s=s, s_=s_: e.tensor_scalar(out=s_.ap, in0=identf.ap, scalar1=shift.ap[:, s:s + 1], scalar2=None, op0=ALU.mult),
                          reads=[identf, shift], writes=[s_])
                    pg.op("pe", lambda e, s=s, r_=r_: e.matmul(bcR.ap[:, s * 128:(s + 1) * 128], lhsT=onesf.ap, rhs=r_.ap, start=True, stop=True),
                          reads=[onesf, r_], writes=[bcR], sig=True)
                    pg.op("pe", lambda e, s=s, s_=s_: e.matmul(bcS.ap[:, s * 128:(s + 1) * 128], lhsT=onesf.ap, rhs=s_.ap, start=True, stop=True),
                          reads=[onesf, s_], writes=[bcS], sig=True)
                for c in range(4):
                    cv_ = cT[k][:, c * TT:(c + 1) * TT]
                    pg.op("dve", lambda e, cv_=cv_: e.tensor_tensor(out=cv_, in0=cv_, in1=bcR.ap, op=ALU.mult), reads=[bcR, cTb[k][c]], writes=[cTb[k][c]])
                    pg.op("dve", lambda e, cv_=cv_: e.tensor_tensor(out=cv_, in0=cv_, in1=bcS.ap, op=ALU.add), reads=[bcS, cTb[k][c]], writes=[cTb[k][c]])
                    pg.op("act", lambda e, cv_=cv_, c=c: e.activation(out=convT[k][:, c * TT:(c + 1) * TT], in_=cv_, func=AF.Silu,
                                                                      scale=cv.ap[:, c:c + 1], bias=cv.ap[:, 4 + c:5 + c]),
                          reads=[cTb[k][c], cv], writes=[convTb[k][c]])

            def outproj(t):
                si, tl = tl_list[t]
                k = t % 2
                a = at[t % 4]
                for s in range(NSUB):
                    pss = []
                    for nh in range(2):
                        po = psO[oc[0] % 4]
                        oc[0] += 1
                        pss.append(po)
                        for kc in range(KC):
                            src = a.ap if kc < 4 else convT[k]
                            kk = kc % 4
                            pg.op("pe", lambda e, kc=kc, kk=kk, src=src, nh=nh, po=po, s=s: e.matmul(
                                po.ap, lhsT=src[:, kk * TT + s * 128: kk * TT + (s + 1) * 128],
                                rhs=Wout.ap[:, kc * D + nh * 512: kc * D + (nh + 1) * 512], start=(kc == 0), stop=(kc == KC - 1)),
                                reads=[a, Wout] + convTb[k], writes=[po], sig=(kc == KC - 1))
                    r0 = seq0[si] + tl * TT + s * 128
                    xr_load(ec[0])
                    xr_load(ec[0] + 1)
                    epilogue(pss[0], pss[1], X1, r0, gpost, xr[ec[0] % 3], stts[ec[0] % 2], junk, X2, None, preloaded=True)
                    ec[0] += 1

            dload(0)
            dload(1)
            pg.dma("pool", Wout.ap.rearrange("p (k n) -> p k n", n=D), wout_d.rearrange("(k p) n -> p k n", p=128), writes=[Wout])
            conv(0)
            for t in range(NTD + 1):
                if t < NTD:
                    stats(t)
                if t + 1 < NTD:
                    conv(t + 1)
                if t < NTD:
                    bc_norm(t)
                if t >= 1:
                    outproj(t - 1)
            pg.barrier()

        cs_mark = CS.off
        if "A" in phases:
            ffn_phase(x_d, wgu1_d, wd1_d, 0, 0, X1, False)
        if "B" in phases:
            phase_b()
        if "C" in phases:
            phase_c()
        if "D" in phases:
            phase_d()
        if "E" in phases:
            ffn_phase(X2, wgu2_d, wd2_d, 2, 2, y_d, True)
        pg.barrier()
        with nc.Block() as block:
            pg.replay(block)
    return nc


def host_consts():
    ident = np.eye(128, dtype=np.float32)
    p = np.arange(128)[:, None, None]
    kt = np.arange(2)[None, :, None]
    q = np.arange(128)[None, None, :]
    rel = (p - 64 + 128 * kt) - q
    arel = np.abs(rel).astype(np.float32)
    slopes = 2.0 ** (-8.0 * np.arange(1, NH + 1, dtype=np.float32) / NH)
    tab = np.zeros((128, 3, NH, 2, 128), np.float32)
    for di, d in enumerate(DILS):
        for h in range(NH):
            b = -(slopes[h] * d) * arel
            tab[:, di, h] = np.where(arel <= 64, b, MASKV)
    return ident, tab.reshape(128, 3 * NH * 256).astype(ml_dtypes.bfloat16)


def make_common(ffn1_pre_g, ffn1_w_gu, ffn1_w_down, ffn1_post_g, mix_pre_g, w_in, conv_w, conv_b, conv_ln_g, conv_ln_b,
                w_out, mix_post_g, ffn2_pre_g, ffn2_w_gu, ffn2_w_down, ffn2_post_g, final_g):
    f = lambda a: np.ascontiguousarray(np.asarray(a, dtype=np.float32))
    ident, biasT = host_consts()
    gT = np.concatenate([f(g).reshape(KC, 128).T for g in (ffn1_pre_g, mix_pre_g, ffn2_pre_g)], axis=1)
    gpost = np.stack([f(g).reshape(D) for g in (ffn1_post_g, mix_post_g, ffn2_post_g, final_g)], axis=0)
    convv = np.concatenate([f(conv_ln_g).reshape(4, 128).T, f(conv_ln_b).reshape(4, 128).T, f(conv_b).reshape(4, 128).T], axis=1)
    return {
        "ffn1_w_gu": f(ffn1_w_gu).reshape(D, 2 * DFF), "ffn1_w_down": f(ffn1_w_down).reshape(DFF, D),
        "ffn2_w_gu": f(ffn2_w_gu).reshape(D, 2 * DFF), "ffn2_w_down": f(ffn2_w_down).reshape(DFF, D),
        "w_in": f(w_in).reshape(D, INW), "w_out": f(w_out).reshape(D, D),
        "gT": f(gT), "gpost": f(gpost), "convw_t": f(f(conv_w).reshape(CK, 512).T), "convv": f(convv),
        "conv_b": f(conv_b).reshape(512), "ident": ident, "biasT": biasT,
    }


_NC_CACHE = {}


def kernel(x_prompt, x_sample, **w):
    xp = np.asarray(x_prompt, dtype=np.float32)
    xs = np.asarray(x_sample, dtype=np.float32)
    n = 8
    common = make_common(**w)
    if "nc" not in _NC_CACHE:
        _NC_CACHE["nc"] = build()
    nc = _NC_CACHE["nc"]
    in_maps = []
    for i in range(n):
        m = dict(common)
        m["x"] = np.concatenate([xp[i], xs[i]], axis=0)
        in_maps.append(m)
    res = run_bass_kernel_spmd(nc, in_maps, core_ids=list(range(n)))
    ys = [np.asarray(r["y"]) for r in res.results]
    y_prompt = np.stack([y[:2048] for y in ys], axis=0).astype(np.float32)
    y_sample = np.stack([y[2048:] for y in ys], axis=0).astype(np.float32)
    return (y_prompt, y_sample)
```
